# Optimizing a Trainium2 kernel written in Bass

```python
import jax, jax.numpy as jnp
from jax import lax
import numpy as np

D_MODEL = 1024
BATCH = 8
SEQ = 2048
DEPTH = 1

GDN_HEADS = 8
GDN_HEAD_DIM = 64
FOX_HEADS = 8
FOX_HEAD_DIM = 64
GDN_WIDTH = GDN_HEADS * GDN_HEAD_DIM
FOX_WIDTH = FOX_HEADS * FOX_HEAD_DIM
D_MIX = GDN_WIDTH + FOX_WIDTH
CONV_K = 4
CHUNK = 64
Q_BLOCK = 128
D_FF = -(-8 * D_MODEL // (3 * 256)) * 256
EPS = 1e-6

SPLIT_SIZES = [
    GDN_WIDTH, GDN_WIDTH, GDN_WIDTH,
    GDN_WIDTH,
    GDN_HEADS, GDN_HEADS,
    FOX_WIDTH, FOX_WIDTH, FOX_WIDTH,
    FOX_WIDTH,
    FOX_HEADS,
]
D_IN = sum(SPLIT_SIZES)
SPLIT_POINTS = list(np.cumsum(SPLIT_SIZES)[:-1])

kernel_name = "hymba_gdn_fox_swiglu"


def rms_norm(x, w):
    xf = x.astype(jnp.float32)
    out = xf * lax.rsqrt(jnp.mean(xf * xf, axis=-1, keepdims=True) + EPS)
    return (out * w.astype(jnp.float32)).astype(x.dtype)


def l2_norm(x):
    xf = x.astype(jnp.float32)
    return xf * lax.rsqrt(jnp.sum(xf * xf, axis=-1, keepdims=True) + EPS)


def causal_depthwise_conv(x, w):
    c = x.shape[-1]
    return lax.conv_general_dilated(
        x, w.reshape(CONV_K, 1, c).astype(x.dtype), window_strides=(1,),
        padding=[(CONV_K - 1, 0)], dimension_numbers=("NWC", "WIO", "NWC"),
        feature_group_count=c)


def gated_delta_rule(q, k, v, beta, g):
    B, T, H, Dk = q.shape
    Dv = v.shape[-1]
    N = T // CHUNK

    def chunks(t):
        return t.reshape(B, N, CHUNK, H, -1).transpose(0, 3, 1, 2, 4)

    q = chunks(q.astype(jnp.float32)) * (Dk ** -0.5)
    k = chunks(k.astype(jnp.float32))
    v = chunks(v.astype(jnp.float32))
    beta = beta.astype(jnp.float32).reshape(B, N, CHUNK, H).transpose(0, 3, 1, 2)
    g = jnp.cumsum(g.astype(jnp.float32).reshape(B, N, CHUNK, H).transpose(0, 3, 1, 2), axis=-1)

    causal = jnp.tril(jnp.ones((CHUNK, CHUNK), dtype=bool))
    strict = jnp.tril(jnp.ones((CHUNK, CHUNK), dtype=bool), k=-1)
    decay = jnp.exp(jnp.where(causal, g[..., :, None] - g[..., None, :], -jnp.inf))

    k_beta = k * beta[..., None]
    v_beta = v * beta[..., None]
    L = jnp.where(strict, jnp.einsum("bhncd,bhnmd->bhncm", k_beta, k) * decay, 0.0)
    eye = jnp.eye(CHUNK, dtype=jnp.float32)
    Tm = lax.linalg.triangular_solve(eye + L, jnp.broadcast_to(eye, L.shape),
                                     left_side=True, lower=True, unit_diagonal=True)
    u = jnp.einsum("bhncm,bhnmd->bhncd", Tm, v_beta)
    w = jnp.einsum("bhncm,bhnmd->bhncd", Tm, k_beta * jnp.exp(g)[..., None])
    intra = jnp.where(causal, jnp.einsum("bhncd,bhnmd->bhncm", q, k) * decay, 0.0)

    def to_scan(t):
        return jnp.moveaxis(t, 2, 0)

    def step(S, xs):
        q_c, k_c, u_c, w_c, A_c, g_c = xs
        v_new = u_c - jnp.einsum("bhcd,bhde->bhce", w_c, S)
        o = (jnp.einsum("bhcd,bhde->bhce", q_c * jnp.exp(g_c)[..., None], S)
             + jnp.einsum("bhcm,bhme->bhce", A_c, v_new))
        g_last = g_c[..., -1]
        S = (S * jnp.exp(g_last)[..., None, None]
             + jnp.einsum("bhcd,bhce->bhde", k_c * jnp.exp(g_last[..., None] - g_c)[..., None], v_new))
        return S, o

    S0 = jnp.zeros((B, H, Dk, Dv), jnp.float32)
    _, o = lax.scan(step, S0, (to_scan(q), to_scan(k), to_scan(u), to_scan(w),
                               to_scan(intra), to_scan(g)))
    return o.transpose(1, 0, 3, 2, 4).reshape(B, T, H, Dv)


def forgetting_attention(q, k, v, log_f):
    B, T, H, D = q.shape
    nb = T // Q_BLOCK
    F = jnp.cumsum(log_f.astype(jnp.float32), axis=1).transpose(0, 2, 1)
    qb = q.reshape(B, nb, Q_BLOCK, H, D).transpose(1, 0, 2, 3, 4)
    Fq = F.reshape(B, H, nb, Q_BLOCK).transpose(2, 0, 1, 3)
    pos_k = jnp.arange(T)
    scale = D ** -0.5

    def block(args):
        i, q_i, F_i = args
        s = jnp.einsum("bqhd,bkhd->bhqk", q_i, k, preferred_element_type=jnp.float32) * scale
        s = s + (F_i[..., :, None] - F[:, :, None, :])
        pos_q = i * Q_BLOCK + jnp.arange(Q_BLOCK)
        s = jnp.where(pos_q[:, None] >= pos_k[None, :], s, -jnp.inf)
        p = jax.nn.softmax(s, axis=-1)
        return jnp.einsum("bhqk,bkhd->bqhd", p.astype(v.dtype), v)

    o = lax.map(block, (jnp.arange(nb), qb, Fq))
    return o.transpose(1, 0, 2, 3, 4).reshape(B, T, H, D)


def setup_inputs(seed: int = 0) -> dict:
    key = jax.random.key(seed)
    ks = jax.random.split(key, 20)
    f32 = jnp.float32

    def normal(k, shape, fan_in):
        return jax.random.normal(k, shape, f32) * (fan_in ** -0.5)

    def gain(k, shape):
        return 1.0 + 0.02 * jax.random.normal(k, shape, f32)

    x = jax.random.normal(ks[0], (BATCH, SEQ, D_MODEL), f32)
    norm1_w = gain(ks[1], (DEPTH, D_MODEL))
    w_in = normal(ks[2], (DEPTH, D_MODEL, D_IN), D_MODEL)
    gdn_conv_w = normal(ks[3], (DEPTH, CONV_K, 3 * GDN_WIDTH), CONV_K)
    gdn_A_log = jnp.log(jax.random.uniform(ks[4], (DEPTH, GDN_HEADS), f32, 1.0, 16.0))
    dt = jnp.exp(jax.random.uniform(ks[5], (DEPTH, GDN_HEADS), f32, np.log(1e-3), np.log(1e-1)))
    gdn_dt_bias = dt + jnp.log(-jnp.expm1(-dt))
    gdn_out_norm_w = gain(ks[6], (DEPTH, GDN_HEAD_DIM))
    fox_f_bias = jax.random.uniform(ks[7], (DEPTH, FOX_HEADS), f32, 1.0, 5.0)
    fox_q_norm_w = gain(ks[8], (DEPTH, FOX_HEAD_DIM))
    fox_k_norm_w = gain(ks[9], (DEPTH, FOX_HEAD_DIM))
    w_out = normal(ks[10], (DEPTH, D_MIX, D_MODEL), D_MIX)
    norm2_w = gain(ks[11], (DEPTH, D_MODEL))
    w_ffn_gate = normal(ks[12], (DEPTH, D_MODEL, D_FF), D_MODEL)
    w_ffn_up = normal(ks[13], (DEPTH, D_MODEL, D_FF), D_MODEL)
    w_ffn_down = normal(ks[14], (DEPTH, D_FF, D_MODEL), D_FF)
    final_norm_w = gain(ks[15], (D_MODEL,))
    return {"x": x, "norm1_w": norm1_w, "w_in": w_in, "gdn_conv_w": gdn_conv_w,
            "gdn_A_log": gdn_A_log, "gdn_dt_bias": gdn_dt_bias,
            "gdn_out_norm_w": gdn_out_norm_w, "fox_f_bias": fox_f_bias,
            "fox_q_norm_w": fox_q_norm_w, "fox_k_norm_w": fox_k_norm_w,
            "w_out": w_out, "norm2_w": norm2_w, "w_ffn_gate": w_ffn_gate,
            "w_ffn_up": w_ffn_up, "w_ffn_down": w_ffn_down, "final_norm_w": final_norm_w}


def reference(x, norm1_w, w_in, gdn_conv_w, gdn_A_log, gdn_dt_bias, gdn_out_norm_w,
              fox_f_bias, fox_q_norm_w, fox_k_norm_w, w_out, norm2_w, w_ffn_gate,
              w_ffn_up, w_ffn_down, final_norm_w):
    B, T, _ = x.shape
    for l in range(DEPTH):
        h = rms_norm(x, norm1_w[l])
        proj = h @ w_in[l]
        (g_q, g_k, g_v, g_z, g_b, g_a,
         f_q, f_k, f_v, f_gate, f_f) = jnp.split(proj, SPLIT_POINTS, axis=-1)

        qkv = jax.nn.silu(causal_depthwise_conv(jnp.concatenate([g_q, g_k, g_v], -1), gdn_conv_w[l]))
        g_q, g_k, g_v = jnp.split(qkv, 3, axis=-1)
        gq = l2_norm(g_q.reshape(B, T, GDN_HEADS, GDN_HEAD_DIM))
        gk = l2_norm(g_k.reshape(B, T, GDN_HEADS, GDN_HEAD_DIM))
        gv = g_v.reshape(B, T, GDN_HEADS, GDN_HEAD_DIM)
        beta = jax.nn.sigmoid(g_b.astype(jnp.float32))
        g_log = -jnp.exp(gdn_A_log[l].astype(jnp.float32)) * jax.nn.softplus(
            g_a.astype(jnp.float32) + gdn_dt_bias[l].astype(jnp.float32))
        o_gdn = gated_delta_rule(gq, gk, gv, beta, g_log)
        z = g_z.reshape(B, T, GDN_HEADS, GDN_HEAD_DIM).astype(jnp.float32)
        o_gdn = rms_norm(o_gdn, gdn_out_norm_w[l]) * jax.nn.silu(z)
        o_gdn = o_gdn.astype(x.dtype).reshape(B, T, GDN_WIDTH)

        fq = rms_norm(f_q.reshape(B, T, FOX_HEADS, FOX_HEAD_DIM), fox_q_norm_w[l])
        fk = rms_norm(f_k.reshape(B, T, FOX_HEADS, FOX_HEAD_DIM), fox_k_norm_w[l])
        fv = f_v.reshape(B, T, FOX_HEADS, FOX_HEAD_DIM)
        log_f = jax.nn.log_sigmoid(f_f.astype(jnp.float32) + fox_f_bias[l].astype(jnp.float32))
        o_fox = forgetting_attention(fq, fk, fv, log_f).reshape(B, T, FOX_WIDTH)
        o_fox = o_fox * jax.nn.sigmoid(f_gate)

        mix = jnp.concatenate([o_gdn, o_fox.astype(x.dtype)], axis=-1)
        x = x + mix @ w_out[l]

        h = rms_norm(x, norm2_w[l])
        x = x + (jax.nn.silu(h @ w_ffn_gate[l]) * (h @ w_ffn_up[l])) @ w_ffn_down[l]
    return rms_norm(x, final_norm_w)
```

```python
import contextlib
import numpy as np
import ml_dtypes
import concourse.bass as bass
import concourse.mybir as mybir
from concourse.bass_utils import run_bass_kernel_spmd

F32 = mybir.dt.float32
BF16 = mybir.dt.bfloat16
AF = mybir.ActivationFunctionType
ALU = mybir.AluOpType
AX = mybir.AxisListType

ENGS = ("pe", "act", "dve", "pool", "sp")
NDMA_SEMS = 32
NHW, NSW = 24, 8
_DTSZ = {F32: 4, BF16: 2}

T = 2048
DM = 1024
NT = 16
KC = 8
DFF = 2816
NFT = 22
EPS = 1e-6
NEG = -30000.0

C_GQ, C_GK, C_GV, C_GZ, C_GB, C_GA = 0, 512, 1024, 1536, 2048, 2056
C_FQ, C_FK, C_FV, C_FG, C_FF = 2064, 2576, 3088, 3600, 4112
DIN = 4120


def _region(ap):
    t = ap.tensor
    if "DRam" in type(t).__name__:
        return None
    sz = _DTSZ[ap.dtype]
    F = 1
    for s in t.shape[1:]:
        F *= int(s)
    off = int(ap.offset)
    p0 = off // F
    f0 = (off % F) * sz
    pn = 1
    ext = 0
    for (st, cnt) in ap.ap:
        st = int(st); cnt = int(cnt)
        if st != 0 and st % F == 0:
            pn = max(pn, (cnt - 1) * (st // F) + 1)
        else:
            ext += (cnt - 1) * abs(st)
    f1 = f0 + (ext + 1) * sz
    if "PSum" in type(t).__name__:
        return (t.name, 0, 128, (f0 // 2048) * 2048, ((f1 + 2047) // 2048) * 2048)
    return (t.name, p0, p0 + pn, f0, f1)


def _overlap(a, b):
    return a[0] == b[0] and a[1] < b[2] and b[1] < a[2] and a[3] < b[4] and b[3] < a[4]


def _contains(a, b):
    return a[0] == b[0] and a[1] <= b[1] and a[2] >= b[2] and a[3] <= b[3] and a[4] >= b[4]


class Op:
    __slots__ = ("eng", "meth", "args", "kw", "deps", "idx", "is_dma", "marked", "seq", "dsem", "dval", "reads", "writes")


class Prog:
    def __init__(self, nc):
        self.nc = nc
        self.ops = []
        self.hist = {}
        self.ndma = 0
        self.dma_ops = {"hw": [], "sw": []}
        self.capture = None

    def I(self, eng, meth, *args, **kw):
        if self.capture is not None:
            self.capture.append((eng, meth, args, kw))
            return None
        op = Op()
        op.eng = eng; op.meth = meth; op.args = args
        op.is_dma = meth == "dma_start"
        op.idx = len(self.ops)
        op.marked = False; op.seq = 0; op.dsem = None; op.dval = 0
        reads = []; writes = []
        extra_r = kw.pop("_reads", None)
        extra_w = kw.pop("_writes", None)
        extra_d = kw.pop("_deps", None)
        op.kw = kw
        for k, v in kw.items():
            if isinstance(v, bass.AP):
                r = _region(v)
                if r is None:
                    continue
                if k in ("out", "accum_out"):
                    writes.append(r)
                else:
                    reads.append(r)
        for v in (extra_r or []):
            r = _region(v)
            if r: reads.append(r)
        for v in (extra_w or []):
            r = _region(v)
            if r: writes.append(r)
        op.reads = reads; op.writes = writes
        deps = set()
        for d_ in (extra_d or []):
            deps.add((d_.idx, 3))
        for r in reads:
            psum_r = r[0] == "PS"
            for (rg, oi, isw) in self.hist.get(r[0], ()):
                if isw and _overlap(rg, r):
                    deps.add((oi, 0))
                elif psum_r and (not isw) and _overlap(rg, r) and self.ops[oi].eng != eng:
                    deps.add((oi, 4))
        for w in writes:
            h = self.hist.get(w[0], [])
            newh = []
            for rec in h:
                (rg, oi, isw) = rec
                if _overlap(rg, w):
                    deps.add((oi, 1 if isw else 2))
                    if _contains(w, rg):
                        continue
                newh.append(rec)
            newh.append((w, op.idx, True))
            self.hist[w[0]] = newh
        for r in reads:
            self.hist.setdefault(r[0], []).append((r, op.idx, False))
        if op.is_dma:
            self.ndma += 1
            kind_ = "sw" if eng == "pool" else "hw"
            lst = self.dma_ops[kind_]
            npool = NSW if kind_ == "sw" else NHW
            j = len(lst)
            if j >= npool:
                deps.add((lst[j - npool].idx, 3))
            lst.append(op)
            op.dsem = (j % npool) + (NHW if kind_ == "sw" else 0)
            op.dval = 16 * (j // npool + 1)
        fd = {}
        for (oi, kind) in deps:
            if oi == op.idx:
                continue
            p = self.ops[oi]
            same = (p.eng == op.eng) and (not p.is_dma) and (not op.is_dma)
            if same and op.eng == "pe":
                continue
            fd[oi] = True
        op.deps = sorted(fd.keys())
        for oi in op.deps:
            self.ops[oi].marked = True
        self.ops.append(op)
        return op

    def mm(self, out, lhsT, rhs, start=True, stop=True):
        return self.I("pe", "matmul", out=out, lhsT=lhsT, rhs=rhs, start=start, stop=stop)

    def tr(self, out, in_, ident):
        return self.I("pe", "transpose", out=out, in_=in_, identity=ident)

    def dma(self, q, out, in_, deps=None):
        return self.I(q, "dma_start", out=out, in_=in_, _deps=deps)

    def act(self, out, in_, func, **kw):
        return self.I("act", "activation", out=out, in_=in_, func=func, **kw)

    def tt(self, eng, out, in0, in1, op):
        return self.I(eng, "tensor_tensor", out=out, in0=in0, in1=in1, op=op)

    def ts(self, eng, out, in0, s1, op0, s2=None, op1=None):
        if op1 is None:
            return self.I(eng, "tensor_scalar", out=out, in0=in0, scalar1=s1, scalar2=None, op0=op0)
        return self.I(eng, "tensor_scalar", out=out, in0=in0, scalar1=s1, scalar2=s2, op0=op0, op1=op1)

    def stt(self, eng, out, in0, scalar, in1, op0, op1):
        return self.I(eng, "scalar_tensor_tensor", out=out, in0=in0, scalar=scalar, in1=in1, op0=op0, op1=op1)

    def cp(self, eng, out, in_):
        if eng == "act":
            return self.I("act", "copy", out=out, in_=in_)
        return self.I(eng, "tensor_copy", out=out, in_=in_)

    def memset(self, eng, ap, val):
        return self.I(eng, "memset", ap, val, _writes=[ap])

    def emit(self, final_wait_ops):
        nc = self.nc
        cnt = {e: 0 for e in ENGS}
        for op in self.ops:
            if op.is_dma:
                continue
            if op.marked:
                cnt[op.eng] += 1
                op.seq = cnt[op.eng]
        self.counts = dict(cnt); self.counts['dma'] = self.ndma; self.counts['ops'] = len(self.ops)
        with contextlib.ExitStack() as es:
            sems = {e: es.enter_context(nc.semaphore("s_" + e)) for e in ENGS}
            dsems = [es.enter_context(nc.semaphore("d%d" % i)) for i in range(NDMA_SEMS)]
            block = es.enter_context(nc.Block())
            ops = self.ops

            def run(engname, E):
                waited = {}
                for op in ops:
                    if op.eng != engname:
                        continue
                    for oi in op.deps:
                        p = ops[oi]
                        if p.is_dma:
                            key = ("d", p.dsem); val = p.dval; sem = dsems[p.dsem]
                        else:
                            key = ("e", p.eng); val = p.seq; sem = sems[p.eng]
                        if waited.get(key, 0) >= val:
                            continue
                        waited[key] = val
                        E.wait_ge(sem, val)
                    ins = getattr(E, op.meth)(*op.args, **op.kw)
                    if op.is_dma:
                        ins.then_inc(dsems[op.dsem], 16)
                    elif op.marked:
                        ins.then_inc(sems[op.eng], 1)
                if engname == "sp":
                    for op in final_wait_ops:
                        E.wait_ge(dsems[op.dsem], op.dval)

            @block.tensor
            def _(E): run("pe", E)

            @block.scalar
            def _(E): run("act", E)

            @block.vector
            def _(E): run("dve", E)

            @block.gpsimd
            def _(E): run("pool", E)

            @block.sync
            def _(E): run("sp", E)


def _consts():
    bf = ml_dtypes.bfloat16
    c = {}
    c["ident_bf"] = np.eye(128, dtype=np.float32).astype(bf)
    t = np.arange(128)
    c["tri_f"] = (t[:, None] <= t[None, :]).astype(np.float32)
    c["ones_f"] = np.ones((128, 128), np.float32)
    key = (np.arange(4)[None, :, None] * 128 + t[:, None, None])
    q = np.arange(512)[None, None, :]
    c["negmask4"] = np.where(key <= q, 0.0, NEG).astype(np.float32).astype(bf).reshape(128, 2048)
    incl = np.where(t[None, :] <= t[:, None], 0.0, NEG).astype(np.float32)
    c["negincl4"] = np.tile(incl, (1, 4)).astype(bf)
    c["strict01"] = (t[None, :] < t[:, None]).astype(np.float32).astype(bf)
    ep = np.zeros((40, 8, 128), np.float32)
    for r in list(range(8)) + list(range(32, 40)):
        h = r % 32
        ep[r, (h % 2) * 4 + h // 2, :] = 1.0
    c["epat"] = ep.reshape(40, 1024).astype(bf)
    c["nepat"] = (-ep).reshape(40, 1024).astype(bf)
    o40 = np.zeros((40, 128), np.float32)
    o40[0:8] = 1.0; o40[32:40] = 1.0
    c["ones40"] = o40.astype(bf)
    c["blk64"] = (t[:, None] // 64 == t[None, :] // 64).astype(np.float32).astype(bf)
    return c


_CONST_SPECS = [("ident_bf", [128, 128], BF16), ("tri_f", [128, 128], F32), ("ones_f", [128, 128], F32),
                ("negmask4", [128, 2048], BF16), ("negincl4", [128, 512], BF16), ("strict01", [128, 128], BF16),
                ("epat", [40, 1024], BF16), ("nepat", [40, 1024], BF16), ("ones40", [40, 128], BF16),
                ("blk64", [128, 128], BF16)]

_PARAM_SPECS = [("n1bc", [128, 1024]), ("n2bc", [128, 1024]), ("nfbc", [128, 1024]), ("convw", [128, 48]),
                ("fbias_bc", [128, 8]), ("alog_bc", [128, 8]), ("dtb_bc", [128, 8]), ("gnorm_bc", [128, 64]),
                ("fqw", [128, 1]), ("fkw", [128, 1])]


NDUMMY = 2
GDUMMY = 0


def build(debug=None, upto=None):
    nc = bass.Bass("TRN2", target_bir_lowering=False)
    dr = {}

    def din(name, shape, dt=F32):
        dr[name] = nc.dram_tensor(name, shape, dt, kind="ExternalInput").ap()
        return dr[name]

    x_d = din("x", [T, DM])
    w_in = din("w_in", [DM, DIN])
    w_out = din("w_out", [DM, DM])
    w_g = din("w_g", [DM, DFF])
    w_u = din("w_u", [DM, DFF])
    w_d = din("w_d", [DFF, DM])
    for (n, s) in _PARAM_SPECS:
        din(n, s)
    for (n, s, dt) in _CONST_SPECS:
        din(n, s, dt)
    out_d = nc.dram_tensor("out", [T, DM], F32, kind="ExternalOutput").ap()
    x1s = nc.dram_tensor("x1s", [T, DM], F32, kind="Internal").ap()
    dbg_d = {}
    for (n, s) in (debug or []):
        dbg_d[n] = nc.dram_tensor(n, s, F32, kind="ExternalOutput").ap()

    w_in_v = w_in.rearrange("(kc p) n -> p kc n", p=128)
    w_out_v = w_out.rearrange("(kc p) n -> p kc n", p=128)
    w_g_v = w_g.rearrange("(kc p) n -> p kc n", p=128)
    w_u_v = w_u.rearrange("(kc p) n -> p kc n", p=128)
    w_d_v = w_d.rearrange("(kc p) n -> p kc n", p=128)

    with contextlib.ExitStack() as es:
        def sb(name, shape, dt):
            return es.enter_context(nc.sbuf_tensor(name, shape, dt))

        P = Prog(nc)
        HT = sb("HT", [128, 16384], BF16)
        BIG1 = sb("BIG1", [128, 16384], BF16)
        BIG2 = sb("BIG2", [128, 16640], BF16)
        FMIX = sb("FMIX", [128, 8192], BF16)
        W16 = sb("W16", [128, 8192], BF16)
        A32 = sb("A32", [128, 8448], F32)
        PTB = sb("PTB", [128, 1536], BF16)
        JNK = sb("JNK", [128, 1024], BF16)
        SM = sb("SM", [128, 2048], F32)
        cst = {}
        for (n, s, dt) in _CONST_SPECS:
            cst[n] = sb("c_" + n, s, dt)
        prm = {}
        for (n, s) in _PARAM_SPECS:
            prm[n] = sb("p_" + n, s, F32)
        PS = es.enter_context(nc.psum_tensor("PS", [128, 4096], F32))

        def bank(b, n=1):
            return PS[:, b * 512:(b + n) * 512]

        def bank_bf(b, n=1):
            return PS[:, b * 512:(b + n) * 512].bitcast(BF16)

        ident = cst["ident_bf"]

        P.dma("sp", out=A32[:, 0:1024], in_=x_d[0:128, :])
        P.dma("act", out=prm["n1bc"][:], in_=dr["n1bc"][:, :])
        P.dma("act", out=cst["ident_bf"][:], in_=dr["ident_bf"][:, :])
        P.dma("sp", out=A32[:, 1024:2048], in_=x_d[128:256, :])
        qs = ["sp", "act"]
        k = 0
        for (n, s, dt) in _CONST_SPECS:
            if n != "ident_bf":
                P.dma(qs[k % 2], out=cst[n][:], in_=dr[n][:, :]); k += 1
        for (n, s) in _PARAM_SPECS:
            if n != "n1bc":
                P.dma(qs[k % 2], out=prm[n][:], in_=dr[n][:, :]); k += 1

        hT = HT[:, :].rearrange("p (kc t) -> p kc t", kc=KC)

        sm_off = [0]

        def smalloc(n):
            o = sm_off[0]
            sm_off[0] += n
            assert sm_off[0] <= 1536
            return SM[:, o:o + n]

        ss1 = smalloc(16); ln1 = smalloc(16); rstd1 = smalloc(16)
        logits = smalloc(NT * 24).rearrange("p (i c) -> p i c", i=NT)
        nlogf = smalloc(NT * 8).rearrange("p (i c) -> p i c", i=NT)
        beta = smalloc(NT * 8).rearrange("p (i c) -> p i c", i=NT)
        nbeta = smalloc(NT * 8).rearrange("p (i c) -> p i c", i=NT)
        nglog = smalloc(NT * 8).rearrange("p (i c) -> p i c", i=NT)
        tmp8a = smalloc(NT * 8).rearrange("p (i c) -> p i c", i=NT)
        tmp8b = smalloc(NT * 8).rearrange("p (i c) -> p i c", i=NT)
        aexp = smalloc(8)
        fqw_s = smalloc(1)

        xst = [A32[:, 0:1024], A32[:, 1024:2048]]
        hb = [JNK[:, :], PTB[:, 0:1024]]
        def a_s1(i):
            xs = xst[i % 2]
            if i >= 2:
                P.dma("sp", out=xs, in_=x_d[i * 128:(i + 1) * 128, :])
            P.act(out=A32[:, 2048:3072], in_=xs, func=AF.Square, accum_out=ss1[:, i:i + 1])
            P.act(out=ln1[:, i:i + 1], in_=ss1[:, i:i + 1], func=AF.Ln, scale=1.0 / DM, bias=EPS)
            P.act(out=rstd1[:, i:i + 1], in_=ln1[:, i:i + 1], func=AF.Exp, scale=-0.5)

        def a_s2(i):
            xs = xst[i % 2]
            h_b = hb[i % 2]
            P.stt("dve", out=h_b, in0=xs, scalar=rstd1[:, i:i + 1], in1=prm["n1bc"][:, :], op0=ALU.mult, op1=ALU.mult)
            pt = bank_bf(i % 2)
            for kc in range(KC):
                P.tr(pt[:, kc * 128:(kc + 1) * 128], h_b[:, kc * 128:(kc + 1) * 128], ident[:, :])
            P.cp("act" if i % 2 == 0 else "dve", out=hT[:, :, i * 128:(i + 1) * 128],
                 in_=pt.rearrange("p (kc t) -> p kc t", kc=KC))

        a_s1(0)
        for i in range(NT):
            if i + 1 < NT:
                a_s1(i + 1)
            a_s2(i)

        wbuf = [W16[:, 0:4096].rearrange("p (kc n) -> p kc n", kc=KC),
                W16[:, 4096:8192].rearrange("p (kc n) -> p kc n", kc=KC)]
        wsm = PTB[:, 1024:1024 + 192].rearrange("p (kc n) -> p kc n", kc=KC)
        P.dma("pool", out=wsm[:, :, 0:16], in_=w_in_v[:, :, C_GB:C_GB + 16])
        P.dma("pool", out=wsm[:, :, 16:24], in_=w_in_v[:, :, C_FF:C_FF + 8])
        P.dma("pool", out=wbuf[0], in_=w_in_v[:, :, C_FV:C_FV + 512])
        P.dma("pool", out=wbuf[1], in_=w_in_v[:, :, C_FG:C_FG + 512])
        for i in range(NT):
            pl = bank(2 + i % 2)[:, 0:24]
            for kc in range(KC):
                P.mm(pl, hT[:, kc, i * 128:(i + 1) * 128], wsm[:, kc, :], start=(kc == 0), stop=(kc == KC - 1))
            P.cp("dve", out=logits[:, i, :], in_=pl)
        fb_bc = prm["fbias_bc"][:, :].unsqueeze(1).broadcast_to([128, NT, 8])
        dtb_bc = prm["dtb_bc"][:, :].unsqueeze(1).broadcast_to([128, NT, 8])
        P.tt("dve", out=tmp8a, in0=logits[:, :, 16:24], in1=fb_bc, op=ALU.add)
        P.act(out=tmp8a, in_=tmp8a, func=AF.Exp, scale=-1.0)
        P.act(out=nlogf, in_=tmp8a, func=AF.Ln, bias=1.0)
        P.act(out=tmp8b, in_=logits[:, :, 0:8], func=AF.Exp, scale=-1.0)
        P.act(out=tmp8b, in_=tmp8b, func=AF.Ln, bias=1.0)
        P.act(out=beta, in_=tmp8b, func=AF.Exp, scale=-1.0)
        P.ts("dve", out=nbeta, in0=beta, s1=-1.0, op0=ALU.mult)
        P.tt("dve", out=tmp8a, in0=logits[:, :, 8:16], in1=dtb_bc, op=ALU.add)
        P.act(out=tmp8a, in_=tmp8a, func=AF.Exp)
        P.act(out=tmp8a, in_=tmp8a, func=AF.Ln, bias=1.0)
        P.act(out=aexp, in_=prm["alog_bc"][:, :], func=AF.Exp)
        P.tt("dve", out=nglog, in0=tmp8a, in1=aexp.unsqueeze(1).broadcast_to([128, NT, 8]), op=ALU.mult)
        P.ts("dve", out=fqw_s, in0=prm["fqw"][:, :], s1=0.125, op0=ALU.mult)

        nF = A32[0:8, 0:2048]
        frow = A32[0:8, 2048:6144].bitcast(BF16)
        KH = frow[:, 0:2048]; KL = frow[:, 2048:4096]; QH = frow[:, 4096:6144]; QL = frow[:, 6144:8192]
        for i in range(NT):
            pf = bank(4 + i % 2)[0:8, 0:128]
            P.mm(pf, nlogf[:, i, :], cst["tri_f"][:, :])
            if i == 0:
                P.cp("dve", out=nF[:, 0:128], in_=pf)
            else:
                P.ts("dve", out=nF[:, i * 128:(i + 1) * 128], in0=pf, s1=nF[:, i * 128 - 1:i * 128], op0=ALU.add)
        P.cp("dve", out=KH, in_=nF)
        P.tt("dve", out=KL, in0=nF, in1=KH, op=ALU.subtract)
        P.ts("dve", out=QH, in0=KH, s1=-1.0, op0=ALU.mult)
        P.ts("dve", out=QL, in0=KL, s1=-1.0, op0=ALU.mult)

        fv_aug = BIG2[:, 0:NT * 8 * 65].rearrange("p (i h e) -> p i h e", i=NT, h=8)
        gate_sig = BIG2[:, 8448:8448 + 8192].rearrange("p (i c) -> p i c", i=NT)
        P.memset("pool", fv_aug[:, :, :, 64:65], 1.0)
        for i in range(NT):
            pv = bank(2 + i % 2)
            for kc in range(KC):
                P.mm(pv, hT[:, kc, i * 128:(i + 1) * 128], wbuf[0][:, kc, :], start=(kc == 0), stop=(kc == KC - 1))
            P.cp("dve", out=fv_aug[:, i, :, 0:64], in_=pv.rearrange("p (h e) -> p h e", h=8))
        for i in range(NT):
            pg = bank(2 + i % 2)
            for kc in range(KC):
                P.mm(pg, hT[:, kc, i * 128:(i + 1) * 128], wbuf[1][:, kc, :], start=(kc == 0), stop=(kc == KC - 1))
            P.act(out=gate_sig[:, i, :], in_=pg, func=AF.Sigmoid)

        P.dma("pool", out=wbuf[0], in_=w_in_v[:, :, C_FQ:C_FQ + 512])
        P.dma("pool", out=wbuf[1], in_=w_in_v[:, :, C_FK:C_FK + 512])
        aug = BIG1[:, :].rearrange("p (s qk a t) -> p s qk a t", s=2, qk=2, a=2)
        P.memset("pool", BIG1[64:68, :], 1.0)
        fmix = FMIX[:, :].rearrange("p (i c) -> p i c", i=NT)
        sqb = PTB[:, 1024:1536]
        lnv = A32[:, 4096:4608] if False else SM[:, 1536:2048]
        rsv = A32[:, 0:512]
        PT = [PTB[:, 0:512], PTB[:, 512:1024]]
        rec = smalloc(8)
        nrec = [0]
        STB = [bank(0), bank(1), bank(2)]
        PT3 = [PTB[:, 0:512], PTB[:, 512:1024], PTB[:, 1024:1536]]
        sqb2 = [JNK[:, 0:512], JNK[:, 512:1024]]
        for hp in range(4):
            slot = hp % 2
            groups = [(qk, tb) for qk in range(2) for tb in range(4)]

            def b3_mm(g):
                qk, tb = groups[g]
                pq = bank(3 + g % 2)
                for kc in range(KC):
                    P.mm(pq, wbuf[qk][:, kc, hp * 128:(hp + 1) * 128], hT[:, kc, tb * 512:(tb + 1) * 512],
                         start=(kc == 0), stop=(kc == KC - 1))

            def b3_sq(g):
                P.act(out=sqb2[g % 2], in_=bank(3 + g % 2), func=AF.Square)

            def b3_ss(g):
                P.mm(bank(5 + g % 2), cst["blk64"][:, :], sqb2[g % 2])

            def b3_fin(g):
                qk, tb = groups[g]
                pq = bank(3 + g % 2); pss = bank(5 + g % 2)
                nw = fqw_s if qk == 0 else prm["fkw"][:, :]
                P.act(out=pss, in_=pss, func=AF.Ln, scale=1.0 / 64, bias=EPS)
                P.act(out=lnv, in_=pss, func=AF.Exp, scale=-0.5)
                for a in range(2):
                    P.stt("dve", out=aug[0:64, slot, qk, a, tb * 512:(tb + 1) * 512],
                          in0=pq[a * 64:(a + 1) * 64, :], scalar=nw[a * 64:(a + 1) * 64, 0:1],
                          in1=lnv[a * 64:(a + 1) * 64, :], op0=ALU.mult, op1=ALU.mult)

            ng_ = len(groups)
            b3_mm(0); b3_sq(0)
            for g in range(ng_):
                if g + 1 < ng_:
                    b3_mm(g + 1)
                b3_ss(g)
                if g + 1 < ng_:
                    b3_sq(g + 1)
                b3_fin(g)
            for a in range(2):
                h = hp * 2 + a
                P.dma("sp", out=aug[64:65, slot, 1, a, :], in_=KH[h:h + 1, :])
                P.dma("sp", out=aug[65:66, slot, 1, a, :], in_=KL[h:h + 1, :])
                P.dma("sp", out=aug[66:67, slot, 0, a, :], in_=QH[h:h + 1, :])
                P.dma("sp", out=aug[67:68, slot, 0, a, :], in_=QL[h:h + 1, :])
            items = [(a, I, j) for a in range(2) for I in range(4) for j in range(4 * I + 4)]

            def stS(idx):
                a, I, j = items[idx]
                qa = aug[0:68, slot, 0, a, :]; ka = aug[0:68, slot, 1, a, :]
                st = STB[idx % 3]
                r = max(0, j - 4 * I)
                diag = j >= 4 * I
                P.mm(st[:, r * 128:512], ka[:, j * 128:(j + 1) * 128], qa[:, I * 512 + r * 128:(I + 1) * 512], start=True, stop=not diag)
                if diag:
                    P.mm(st[:, r * 128:(r + 1) * 128], ident[:, :], cst["negmask4"][:, 0:128], start=False, stop=True)

            def stE(idx):
                a, I, j = items[idx]
                r = max(0, j - 4 * I)
                P.act(out=PT3[idx % 3][:, r * 128:512], in_=STB[idx % 3][:, r * 128:512], func=AF.Exp)

            def stV(idx):
                a, I, j = items[idx]
                h = hp * 2 + a
                pt_ = PT3[idx % 3]
                for ip in range(max(j, 4 * I), 4 * I + 4):
                    r = ip - 4 * I
                    po = bank(3 + r)[:, 0:65]
                    P.mm(po, pt_[:, r * 128:(r + 1) * 128], fv_aug[:, j, h, :], start=(j == 0), stop=(j == ip))
                    if j == ip:
                        rc = rec[:, nrec[0] % 8:nrec[0] % 8 + 1]
                        nrec[0] += 1
                        P.I("dve", "reciprocal", out=rc, in_=po[:, 64:65])
                        P.stt("dve", out=fmix[:, ip, h * 64:(h + 1) * 64], in0=po[:, 0:64], scalar=rc,
                              in1=gate_sig[:, ip, h * 64:(h + 1) * 64], op0=ALU.mult, op1=ALU.mult)

            n_ = len(items)
            for idx in range(n_ + 2):
                if idx < n_:
                    stS(idx)
                if 0 <= idx - 1 < n_:
                    stE(idx - 1)
                if 0 <= idx - 2 < n_:
                    stV(idx - 2)
                for _ in range(NDUMMY):
                    P.mm(bank(7)[:, 0:256], ident[:, :], cst["negmask4"][:, 0:256])

        qnT = BIG1[:, 0:8192].rearrange("p (c t) -> p c t", c=4)
        knT = BIG1[:, 8192:16384].rearrange("p (c t) -> p c t", c=4)
        vT = BIG2[:, 0:8192].rearrange("p (c t) -> p c t", c=4)
        zs_tok = BIG2[:, 8192:16384].rearrange("p (i c) -> p i c", i=NT)
        P.dma("pool", out=wbuf[0], in_=w_in_v[:, :, C_GZ:C_GZ + 512])
        P.dma("pool", out=wbuf[1], in_=w_in_v[:, :, C_GQ:C_GQ + 512])
        for i in range(NT if upto != 'B' else 0):
            pz = bank(i % 2)
            for kc in range(KC):
                P.mm(pz, hT[:, kc, i * 128:(i + 1) * 128], wbuf[0][:, kc, :], start=(kc == 0), stop=(kc == KC - 1))
            P.act(out=zs_tok[:, i, :], in_=pz, func=AF.Silu)
            P.tt("pool", out=zs_tok[:, i, :].rearrange("p (h e) -> p h e", h=8), in0=zs_tok[:, i, :].rearrange("p (h e) -> p h e", h=8),
                 in1=prm["gnorm_bc"][:, :].unsqueeze(1).broadcast_to([128, 8, 64]), op=ALU.mult)
        raws = [A32[:, 0:2051], A32[:, 2051:4102]]
        accs = [A32[:, 4102:6150], A32[:, 6150:8198]]
        P.memset("pool", A32[:, 0:3], 0.0)
        P.memset("pool", A32[:, 2051:2054], 0.0)
        cw = prm["convw"]
        NCT = 12 if upto != 'B' else 0

        def c2_s1c(ct, tb):
            grp = ct // 4
            wb = wbuf[(grp + 1) % 2]
            if tb == 0 and ct == 0:
                P.dma("pool", out=wbuf[0], in_=w_in_v[:, :, C_GK:C_GK + 512])
            if tb == 0 and ct == 4:
                P.dma("pool", out=wbuf[1], in_=w_in_v[:, :, C_GV:C_GV + 512])
            c0 = (ct % 4) * 128
            raw = raws[ct % 2]
            pp = bank(2 + tb % 2)
            for kc in range(KC):
                P.mm(pp, wb[:, kc, c0:c0 + 128], hT[:, kc, tb * 512:(tb + 1) * 512], start=(kc == 0), stop=(kc == KC - 1))
            P.cp("act", out=raw[:, 3 + tb * 512:3 + (tb + 1) * 512], in_=pp)

        def c2_tap(ct, n):
            raw = raws[ct % 2]; acc = accs[ct % 2]
            if n == 0:
                P.ts("dve", out=acc, in0=raw[:, 3:2051], s1=cw[:, ct * 4 + 3:ct * 4 + 4], op0=ALU.mult)
            else:
                kk = 3 - n
                P.stt("dve", out=acc, in0=raw[:, kk:kk + 2048], scalar=cw[:, ct * 4 + kk:ct * 4 + kk + 1],
                      in1=acc, op0=ALU.mult, op1=ALU.add)

        def c2_silu(ct):
            acc = accs[ct % 2]
            if ct >= 8:
                P.act(out=vT[:, ct - 8, :], in_=acc, func=AF.Silu)
            else:
                P.act(out=acc, in_=acc, func=AF.Silu)

        def n_sq(ct, tb):
            P.act(out=sqb2[tb % 2], in_=accs[ct % 2][:, tb * 512:(tb + 1) * 512], func=AF.Square)

        def n_ss(ct, tb):
            P.mm(bank(4 + tb % 2), cst["blk64"][:, :], sqb2[tb % 2])

        def n_fin(ct, tb):
            acc = accs[ct % 2]
            sl = slice(tb * 512, (tb + 1) * 512)
            pss = bank(4 + tb % 2)
            P.act(out=pss, in_=pss, func=AF.Ln, bias=EPS)
            P.act(out=pss, in_=pss, func=AF.Exp, scale=-0.5)
            if ct < 4:
                P.stt("dve", out=qnT[:, ct, sl], in0=acc[:, sl], scalar=0.125, in1=pss, op0=ALU.mult, op1=ALU.mult)
            else:
                P.tt("dve", out=knT[:, ct - 4, sl], in0=acc[:, sl], in1=pss, op=ALU.mult)

        for it in range(NCT + 2 if NCT else 0):
            cn = it - 1; nn = it - 2
            nv = 0 <= nn < 8
            if nv:
                n_sq(nn, 0)
            for tb in range(4):
                if it < NCT:
                    c2_s1c(it, tb)
                if nv:
                    n_ss(nn, tb)
                    if tb < 3:
                        n_sq(nn, tb + 1)
                    n_fin(nn, tb)
                if 0 <= cn < NCT:
                    c2_tap(cn, tb)
            if 0 <= cn < NCT:
                c2_silu(cn)

        NTI = NT if upto not in ('C2', 'B') else 0
        if upto and upto.startswith('C3:'):
            NTI = int(upto.split(':')[1])
        wout = W16[:, :].rearrange("p (kc n) -> p kc n", kc=KC)
        P.dma("pool", out=wout, in_=w_out_v[:, :, :])

        def htv(o, n):
            return HT[:, o:o + n]
        KBG = htv(0, 512); KD = htv(512, 512); VB = htv(1024, 512)
        GS = HT[0:40, 1536:1664]; GEXP = HT[0:40, 1664:2688]
        DINC = htv(2688, 1024); DS = htv(3712, 1024); Am = htv(4736, 1024); AT = htv(5760, 1024)
        Pbuf = [htv(6784, 1024), htv(7808, 1024)]
        Qbuf = [htv(8832, 1024), htv(9856, 1024)]
        Ybuf = [htv(10880, 1024), htv(11904, 1024)]
        WT = htv(12928, 512); VNEW = htv(13440, 512); SBF = htv(13952, 256)
        GM = htv(14208, 512); MIXT = htv(14720, 1024)
        xs_t = A32[:, 0:1024]; x1o = A32[:, 1024:2048]
        u32 = A32[:, 2048:2560]; o32 = A32[:, 2560:3072]; otmp = A32[:, 3072:3584]
        S32 = A32[:, 3584:3840]
        ngpad = A32[:, 3840:4480].rearrange("p (i c) -> p i c", i=NT)
        ng = smalloc(8); dlt = smalloc(8); eg = smalloc(8); edl = smalloc(8); egl = smalloc(8); bg = smalloc(8)
        ssq = smalloc(8); lno = smalloc(8)
        P.memset("pool", A32[:, 3840:4480], 0.0)
        P.cp("pool", out=ngpad[:, :, 0:8], in_=nglog)
        P.cp("pool", out=ngpad[:, :, 32:40], in_=nglog)
        P.memset("pool", S32, 0.0)
        P.memset("pool", SBF, 0.0)

        def HX(h):
            return (h % 2) * 4 + h // 2

        def v8(ap, e):
            return ap.rearrange("p (h e) -> p h e", h=8)

        def bc_last(ap8, e):
            return ap8.unsqueeze(2).broadcast_to([128, 8, e])

        x1w = {}
        SUB = int(upto.split(':')[2]) if (upto and upto.count(':') == 2) else 99

        def gdn_tile(i):
            tsl = slice(i * 128, (i + 1) * 128)
            kv = bank_bf(0)
            for hp in range(4):
                P.tr(kv[:, hp * 128:(hp + 1) * 128], knT[:, hp, tsl], ident[:, :])
            for hp in range(4):
                P.tr(kv[:, 512 + hp * 128:512 + (hp + 1) * 128], vT[:, hp, tsl], ident[:, :])
            kps = kv[:, 0:512]; vps = kv[:, 512:1024]
            if SUB < 2:
                return
            b1 = bank(1)
            pg = b1[:, 0:8]; pgl = b1[:, 8:16]; pgt = b1[0:40, 128:256]
            P.mm(pg, cst["tri_f"][:, :], nglog[:, i, :])
            P.mm(pgl, cst["ones_f"][:, :], nglog[:, i, :])
            P.mm(pgt, ngpad[:, i, :], cst["tri_f"][:, :])
            P.cp("dve", out=ng, in_=pg)
            P.act(out=eg, in_=pg, func=AF.Exp, scale=-1.0)
            P.tt("dve", out=dlt, in0=pgl, in1=ng, op=ALU.subtract)
            P.act(out=edl, in_=dlt, func=AF.Exp, scale=-1.0)
            P.act(out=egl, in_=pgl, func=AF.Exp, scale=-1.0)
            P.tt("dve", out=bg, in0=beta[:, i, :], in1=eg, op=ALU.mult)
            if SUB < 3:
                return
            P.tt("dve", out=v8(KBG, 64), in0=v8(kps, 64), in1=bc_last(bg, 64), op=ALU.mult)
            P.tt("dve", out=v8(KD, 64), in0=v8(kps, 64), in1=bc_last(edl, 64), op=ALU.mult)
            P.tt("dve", out=v8(VB, 64), in0=v8(vps, 64), in1=bc_last(beta[:, i, :], 64), op=ALU.mult)
            if SUB < 4:
                return
            P.cp("dve", out=GS, in_=pgt)
            P.tt("dve", out=GS[32:40, :], in0=pgt[32:40, :], in1=GS[32:40, :], op=ALU.subtract)
            P.tt("dve", out=GEXP.rearrange("p (h m) -> p h m", h=8), in0=GS.unsqueeze(1).broadcast_to([40, 8, 128]),
                 in1=cst["epat"][:, :].rearrange("p (h m) -> p h m", h=8), op=ALU.mult)
            ds4 = DS.rearrange("p (a hp m) -> p a hp m", a=2, hp=4)
            nb4 = nbeta[:, i, :].rearrange("p (hp a) -> p a hp", a=2).unsqueeze(3).broadcast_to([128, 2, 4, 128])
            P.tt("pool", out=ds4, in0=cst["strict01"][:, :].unsqueeze(1).unsqueeze(1).broadcast_to([128, 2, 4, 128]), in1=nb4, op=ALU.mult)
            if SUB < 5:
                return
            yield
            for b in range(2):
                pd = bank(2 + b)
                P.mm(pd, GS, cst["nepat"][:, b * 512:(b + 1) * 512], start=True, stop=False)
                P.mm(pd, cst["ones40"][:, :], GEXP[:, b * 512:(b + 1) * 512], start=False, stop=False)
                P.mm(pd, ident[:, :], cst["negincl4"][:, :], start=False, stop=True)
            P.act(out=DINC, in_=bank(2, 2), func=AF.Exp)
            if SUB < 6:
                return
            pG = bank(4, 2); pQK = bank(6, 2)
            for h in range(8):
                hr = slice((h % 2) * 64, (h % 2) * 64 + 64)
                P.mm(pG[:, HX(h) * 128:(HX(h) + 1) * 128], knT[hr, h // 2, tsl], knT[hr, h // 2, tsl])
            for h in range(8):
                hr = slice((h % 2) * 64, (h % 2) * 64 + 64)
                P.mm(pQK[:, HX(h) * 128:(HX(h) + 1) * 128], qnT[hr, h // 2, tsl], knT[hr, h // 2, tsl])
            if SUB < 7:
                return
            P.tt("dve", out=Am, in0=pQK, in1=DINC, op=ALU.mult)
            P.tt("dve", out=DS, in0=DS, in1=DINC, op=ALU.mult)
            Pc, Pn = Pbuf[0], Pbuf[1]
            Qc, Qn = Qbuf[0], Qbuf[1]
            Yc, Yn = Ybuf[0], Ybuf[1]
            P.tt("dve", out=Pc, in0=pG, in1=DS, op=ALU.mult)
            blk8 = cst["blk64"][:, :].unsqueeze(1).broadcast_to([128, 8, 128])
            P.tt("dve", out=v8(Pn, 128), in0=v8(Pc, 128), in1=blk8, op=ALU.mult)
            POFF = DINC
            P.tt("dve", out=POFF, in0=Pc, in1=Pn, op=ALU.subtract)
            Pc, Pn = Pn, Pc
            pt0 = bank_bf(0); pt1 = bank_bf(1)
            for h in range(8):
                P.tr(pt0[:, h * 128:(h + 1) * 128], Am[:, h * 128:(h + 1) * 128], ident[:, :])
            P.cp("act", out=AT, in_=pt0)
            for h in range(8):
                P.tr(pt1[:, h * 128:(h + 1) * 128], Pc[:, h * 128:(h + 1) * 128], ident[:, :])
            P.cp("act", out=Qc, in_=pt1)
            P.tt("dve", out=v8(Yc, 128), in0=v8(pt1, 128), in1=ident[:, :].unsqueeze(1).broadcast_to([128, 8, 128]), op=ALU.add)
            if SUB < 8:
                return
            for k in range(6):
                needQ = k <= 3
                needP = k <= 4
                needY = k >= 1
                for b in range(2):
                    bq = bank(2 + 3 * b); bp = bank(3 + 3 * b); by = bank(4 + 3 * b)
                    hs = slice(b * 512, (b + 1) * 512)
                    for hh in range(4):
                        h = 4 * b + hh
                        s_ = slice(h * 128, (h + 1) * 128)
                        o_ = slice(hh * 128, (hh + 1) * 128)
                        if needQ:
                            P.mm(bq[:, o_], Pc[:, s_], Qc[:, s_])
                        if needY:
                            P.mm(by[:, o_], Pc[:, s_], Yc[:, s_])
                        if needP:
                            P.mm(bp[:, o_], Qc[:, s_], Pc[:, s_])
                    for _ in range(GDUMMY):
                        P.mm(bank(0)[:, 0:128], ident[:, :], cst["negmask4"][:, 0:128])
                    if needQ:
                        P.cp("act", out=Qn[:, hs], in_=bq)
                    if needP:
                        P.cp("act", out=Pn[:, hs], in_=bp)
                    if needY:
                        P.tt("dve", out=Yn[:, hs], in0=by, in1=Yc[:, hs], op=ALU.add)
                if needQ:
                    Qc, Qn = Qn, Qc
                if needP:
                    Pc, Pn = Pn, Pc
                if needY:
                    Yc, Yn = Yn, Yc
            Ybd = Yc
            ptb = bank_bf(2)
            for h in range(8):
                P.tr(ptb[:, h * 128:(h + 1) * 128], Ybd[:, h * 128:(h + 1) * 128], ident[:, :])
            TBD = Qbuf[0]; M1 = Qbuf[1]
            pm1 = bank(4, 2); pm2 = bank(6, 2)
            for b in range(2):
                hs = slice(b * 512, (b + 1) * 512)
                P.cp("dve", out=TBD[:, hs], in_=ptb[:, hs])
                for hh in range(4):
                    s_ = slice((4 * b + hh) * 128, (4 * b + hh + 1) * 128)
                    P.mm(pm1[:, s_], POFF[:, s_], Ybd[:, s_])
                P.cp("act", out=M1[:, hs], in_=pm1[:, hs])
            for b in range(2):
                hs = slice(b * 512, (b + 1) * 512)
                for hh in range(4):
                    s_ = slice((4 * b + hh) * 128, (4 * b + hh + 1) * 128)
                    P.mm(pm2[:, s_], TBD[:, s_], M1[:, s_])
                P.tt("dve", out=Ybd[:, hs], in0=pm2[:, hs], in1=Ybd[:, hs], op=ALU.add)
            Y = Ybd
            if SUB < 9:
                return
            pu = bank(0); pw = bank(1)
            for h in range(8):
                P.mm(pu[:, h * 64:(h + 1) * 64], Y[:, HX(h) * 128:(HX(h) + 1) * 128], VB[:, h * 64:(h + 1) * 64])
            P.cp("act", out=u32, in_=pu)
            for h in (0, 2, 4, 6, 1, 3, 5, 7):
                hr = slice((h % 2) * 64, (h % 2) * 64 + 64)
                P.mm(pw[hr, (h // 2) * 128:(h // 2 + 1) * 128], KBG[:, h * 64:(h + 1) * 64], Y[:, HX(h) * 128:(HX(h) + 1) * 128])
            P.cp("act", out=WT, in_=pw)
            if SUB < 10:
                return
            pws = [bank(2)[:, 0:256], bank(3)[:, 0:256]]; po1 = [bank(6)[:, 0:256], bank(7)[:, 0:256]]
            po2 = bank(4); pS = bank(5)[:, 0:256]
            for h in (0, 2, 4, 6, 1, 3, 5, 7):
                hr = slice((h % 2) * 64, (h % 2) * 64 + 64)
                P.mm(pws[h % 2][:, (h // 2) * 64:(h // 2 + 1) * 64], WT[hr, (h // 2) * 128:(h // 2 + 1) * 128], SBF[hr, (h // 2) * 64:(h // 2 + 1) * 64])
            u4 = u32.rearrange("p (hp a e) -> p hp a e", hp=4, a=2)
            vn4 = VNEW.rearrange("p (hp a e) -> p hp a e", hp=4, a=2)
            for a in range(2):
                P.tt("dve", out=vn4[:, :, a, :], in0=u4[:, :, a, :], in1=pws[a].rearrange("p (hp e) -> p hp e", hp=4), op=ALU.subtract)
            for h in (0, 2, 4, 6, 1, 3, 5, 7):
                hr = slice((h % 2) * 64, (h % 2) * 64 + 64)
                P.mm(po1[h % 2][:, (h // 2) * 64:(h // 2 + 1) * 64], qnT[hr, h // 2, tsl], SBF[hr, (h // 2) * 64:(h // 2 + 1) * 64])
            for h in range(8):
                P.mm(po2[:, h * 64:(h + 1) * 64], AT[:, HX(h) * 128:(HX(h) + 1) * 128], VNEW[:, h * 64:(h + 1) * 64])
            for h in (0, 2, 4, 6, 1, 3, 5, 7):
                hr = slice((h % 2) * 64, (h % 2) * 64 + 64)
                P.mm(pS[hr, (h // 2) * 64:(h // 2 + 1) * 64], KD[:, h * 64:(h + 1) * 64], VNEW[:, h * 64:(h + 1) * 64])
            ot4 = otmp.rearrange("p (hp a e) -> p hp a e", hp=4, a=2)
            eg4 = eg.rearrange("p (hp a) -> p hp a", a=2)
            for a in range(2):
                P.tt("dve", out=ot4[:, :, a, :], in0=po1[a].rearrange("p (hp e) -> p hp e", hp=4),
                     in1=eg4[:, :, a].unsqueeze(2).broadcast_to([128, 4, 64]), op=ALU.mult)
            P.tt("dve", out=o32, in0=otmp, in1=po2, op=ALU.add)
            eglv = egl.rearrange("p (hp a) -> p hp a", a=2)
            for a in range(2):
                pr = slice(a * 64, (a + 1) * 64)
                P.tt("pool", out=S32[pr, :].rearrange("p (hp e) -> p hp e", hp=4), in0=S32[pr, :].rearrange("p (hp e) -> p hp e", hp=4),
                     in1=eglv[pr, :, a].unsqueeze(2).broadcast_to([64, 4, 64]), op=ALU.mult)
            P.tt("dve", out=S32, in0=S32, in1=pS, op=ALU.add)
            P.cp("act", out=SBF, in_=S32)
            if SUB < 11:
                return
            yield
            P.tt("dve", out=otmp, in0=o32, in1=o32, op=ALU.mult)
            P.I("dve", "reduce_sum", out=ssq, in_=v8(otmp, 64), axis=AX.X)
            P.act(out=lno, in_=ssq, func=AF.Ln, scale=1.0 / 64, bias=EPS)
            P.act(out=lno, in_=lno, func=AF.Exp, scale=-0.5)
            P.tt("dve", out=v8(otmp, 64), in0=v8(o32, 64), in1=bc_last(lno, 64), op=ALU.mult)
            P.tt("dve", out=GM, in0=otmp, in1=zs_tok[:, i, :], op=ALU.mult)
            if SUB < 12:
                return
            yield
            pm = bank_bf(2)
            for kc in range(4):
                P.tr(pm[:, kc * 128:(kc + 1) * 128], GM[:, kc * 128:(kc + 1) * 128], ident[:, :])
            for kc in range(4):
                P.tr(pm[:, (4 + kc) * 128:(5 + kc) * 128], fmix[:, i, kc * 128:(kc + 1) * 128], ident[:, :])
            P.cp("act", out=MIXT, in_=pm)
            P.dma("sp", out=xs_t, in_=x_d[i * 128:(i + 1) * 128, :])
            py = bank(6, 2)
            for nb in range(2):
                for kc in range(KC):
                    P.mm(py[:, nb * 512:(nb + 1) * 512], MIXT[:, kc * 128:(kc + 1) * 128], wout[:, kc, nb * 512:(nb + 1) * 512],
                         start=(kc == 0), stop=(kc == KC - 1))
            P.tt("dve", out=x1o, in0=py, in1=xs_t, op=ALU.add)
            x1w[i] = P.dma("sp", out=x1s[i * 128:(i + 1) * 128, :], in_=x1o)
            if debug and any(n == "d_gm" for (n, s) in debug):
                stg = A32[:, 4480:4992]
                P.cp("dve", out=stg, in_=GM)
                P.dma("sp", out=dbg_d["d_gm"][i * 128:(i + 1) * 128, :], in_=stg)
                stg2 = A32[:, 4992:5504]
                P.cp("dve", out=stg2, in_=o32)
                P.dma("sp", out=dbg_d["d_o32"][i * 128:(i + 1) * 128, :], in_=stg2)

        gens = [gdn_tile(i) for i in range(NTI)]

        def adv(g):
            try:
                next(g)
            except StopIteration:
                pass

        if NTI:
            adv(gens[0])
        for i in range(NTI):
            adv(gens[i])
            P.capture = []; adv(gens[i]); la = P.capture
            lb = []
            if i + 1 < NTI:
                P.capture = []; adv(gens[i + 1]); lb = P.capture
            P.capture = None
            for k_ in range(max(len(la), len(lb))):
                if k_ < len(lb):
                    e_, m_, a_, kw_ = lb[k_]; P.I(e_, m_, *a_, **kw_)
                if k_ < len(la):
                    e_, m_, a_, kw_ = la[k_]; P.I(e_, m_, *a_, **kw_)
            adv(gens[i])

        X1B = A32[:, 0:8192].rearrange("p (t c) -> p t c", t=8)
        OUTB = FMIX[:, :].bitcast(F32)[:, 0:1024]
        H2T = W16[:, :].rearrange("p (kc t) -> p kc t", kc=KC)

        def ACTT(f):
            if f < 16:
                return HT[:, f * 1024:(f + 1) * 1024]
            return BIG2[:, (f - 16) * 1024:(f - 15) * 1024]
        wgb = [BIG1[:, 0:4096].rearrange("p (kc n) -> p kc n", kc=KC), BIG1[:, 8192:12288].rearrange("p (kc n) -> p kc n", kc=KC)]
        wub = [BIG1[:, 4096:8192].rearrange("p (kc n) -> p kc n", kc=KC), BIG1[:, 12288:16384].rearrange("p (kc n) -> p kc n", kc=KC)]
        wdk = [BIG2[:, 6144:10240].rearrange("p (kc n) -> p kc n", kc=8), BIG2[:, 10240:14336].rearrange("p (kc n) -> p kc n", kc=8),
               FMIX[:, 2048:5120].rearrange("p (kc n) -> p kc n", kc=6)]
        sgs2 = [SM[:, 1536:2048], FMIX[:, :].bitcast(F32)[:, 2560:3072]]
        ss2 = smalloc(16); ln2 = smalloc(16); ss3 = smalloc(16); ln3 = smalloc(16)
        final_ops = []
        cntg = 0; cntp = 0
        XSTG = FMIX[:, :].bitcast(F32)[:, 3072:4096]

        def f_s1(tb_, t8, src):
            i = tb_ * 8 + t8
            P.act(out=hb[i % 2], in_=src, func=AF.Square, accum_out=ss2[:, i:i + 1])
            P.act(out=ln2[:, i:i + 1], in_=ss2[:, i:i + 1], func=AF.Ln, scale=1.0 / DM, bias=EPS)
            P.act(out=ln2[:, i:i + 1], in_=ln2[:, i:i + 1], func=AF.Exp, scale=-0.5)

        def f_s2(tb_, t8, src):
            i = tb_ * 8 + t8
            h_b = hb[i % 2]
            P.stt("dve", out=h_b, in0=src, scalar=ln2[:, i:i + 1], in1=prm["n2bc"][:, :], op0=ALU.mult, op1=ALU.mult)
            pt = bank_bf(i % 2)
            for kc in range(KC):
                P.tr(pt[:, kc * 128:(kc + 1) * 128], h_b[:, kc * 128:(kc + 1) * 128], ident[:, :])
            P.cp("act" if i % 2 == 0 else "dve", out=H2T[:, :, t8 * 128:(t8 + 1) * 128],
                 in_=pt.rearrange("p (kc t) -> p kc t", kc=KC))

        def f_next_tile(t8):
            i = 8 + t8
            P.dma("sp", out=XSTG, in_=x1s[i * 128:(i + 1) * 128, :], deps=[x1w[i]])
            f_s1(1, t8, XSTG)
            f_s2(1, t8, XSTG)

        for tb in range(2 if upto is None else 0):
            for t8 in range(8):
                i = tb * 8 + t8
                P.dma("sp" if t8 % 2 == 0 else "act", out=X1B[:, t8, :], in_=x1s[i * 128:(i + 1) * 128, :], deps=[x1w[i]])
            if tb == 0:
                f_s1(0, 0, X1B[:, 0, :])
                for t8 in range(8):
                    if t8 + 1 < 8:
                        f_s1(0, t8 + 1, X1B[:, t8 + 1, :])
                    f_s2(0, t8, X1B[:, t8, :])
            nxt_tiles = list(range(8)) if tb == 0 else []
            for fg in range(6):
                ncol = 512 if fg < 5 else 256
                wg_ = wgb[cntg % 2]; wu_ = wub[cntg % 2]; cntg += 1
                P.dma("pool", out=wg_[:, :, 0:ncol], in_=w_g_v[:, :, fg * 512:fg * 512 + ncol])
                P.dma("pool", out=wu_[:, :, 0:ncol], in_=w_u_v[:, :, fg * 512:fg * 512 + ncol])
                for ft in range(ncol // 128):
                    f = fg * 4 + ft
                    for th in range(2):
                        tsl_ = slice(th * 512, (th + 1) * 512)
                        psg = bank(0 + 2 * (cntp % 2)); psu = bank(1 + 2 * (cntp % 2))
                        sg_ = sgs2[cntp % 2]
                        cntp += 1
                        for kc in range(KC):
                            P.mm(psg, wg_[:, kc, ft * 128:(ft + 1) * 128], H2T[:, kc, tsl_], start=(kc == 0), stop=(kc == KC - 1))
                        for kc in range(KC):
                            P.mm(psu, wu_[:, kc, ft * 128:(ft + 1) * 128], H2T[:, kc, tsl_], start=(kc == 0), stop=(kc == KC - 1))
                        P.act(out=sg_, in_=psg, func=AF.Silu)
                        P.tt("dve", out=ACTT(f)[:, tsl_], in0=sg_, in1=psu, op=ALU.mult)
            def ffn_final(t8):
                i = tb * 8 + t8
                P.act(out=OUTB, in_=X1B[:, t8, :], func=AF.Square, accum_out=ss3[:, i:i + 1])
                P.act(out=ln3[:, i:i + 1], in_=ss3[:, i:i + 1], func=AF.Ln, scale=1.0 / DM, bias=EPS)
                P.act(out=ln3[:, i:i + 1], in_=ln3[:, i:i + 1], func=AF.Exp, scale=-0.5)
                P.stt("dve", out=OUTB, in0=X1B[:, t8, :], scalar=ln3[:, i:i + 1], in1=prm["nfbc"][:, :], op0=ALU.mult, op1=ALU.mult)
                o = P.dma("sp", out=out_d[i * 128:(i + 1) * 128, :], in_=OUTB)
                final_ops.append(o)

            for nb in range(2):
                for kg in range(3):
                    nk = 8 if kg < 2 else 6
                    P.dma("pool", out=wdk[kg], in_=w_d_v[:, kg * 8:kg * 8 + nk, nb * 512:(nb + 1) * 512])
                for half in range(2):
                    for kg in range(3):
                        nk = 8 if kg < 2 else 6
                        for t4 in range(4):
                            tt_ = half * 4 + t4
                            for j in range(nk):
                                f = kg * 8 + j
                                P.mm(bank(4 + t4), ACTT(f)[:, tt_ * 128:(tt_ + 1) * 128], wdk[kg][:, j, :], start=(f == 0), stop=(f == NFT - 1))
                        if nxt_tiles:
                            f_next_tile(nxt_tiles.pop(0))
                    for t4 in range(4):
                        tt_ = half * 4 + t4
                        P.tt("dve", out=X1B[:, tt_, nb * 512:(nb + 1) * 512], in0=bank(4 + t4), in1=X1B[:, tt_, nb * 512:(nb + 1) * 512], op=ALU.add)
                    if nb == 1:
                        for t4 in range(4):
                            ffn_final(half * 4 + t4)
        if debug:
            for (n, s) in debug:
                if n == "d_fmix":
                    for i in range(NT):
                        stg = A32[:, 5120:5632]
                        P.cp("dve", out=stg, in_=fmix[:, i, :])
                        final_ops.append(P.dma("sp", out=dbg_d[n][i * 128:(i + 1) * 128, :], in_=stg))
                if n == "d_x1":
                    for i in sorted(x1w.keys()):
                        stg = A32[:, 5120:6144]
                        P.dma("sp", out=stg, in_=x1s[i * 128:(i + 1) * 128, :], deps=[x1w[i]])
                        o = P.dma("sp", out=dbg_d[n][i * 128:(i + 1) * 128, :], in_=stg)
                        final_ops.append(o)
        if upto is not None:
            final_ops.append(P.dma('sp', out=out_d[0:128, :], in_=A32[:, 0:1024]))
        P.emit(final_ops)
        nc._prog_counts = P.counts
        nc._prog = P
    return nc


def _host_inputs(inputs):
    f = np.float32
    x = np.ascontiguousarray(inputs["x"], dtype=f)
    shared = {
        "w_in": np.ascontiguousarray(inputs["w_in"][0], dtype=f),
        "w_out": np.ascontiguousarray(inputs["w_out"][0], dtype=f),
        "w_g": np.ascontiguousarray(inputs["w_ffn_gate"][0], dtype=f),
        "w_u": np.ascontiguousarray(inputs["w_ffn_up"][0], dtype=f),
        "w_d": np.ascontiguousarray(inputs["w_ffn_down"][0], dtype=f),
        "n1bc": np.ascontiguousarray(np.broadcast_to(inputs["norm1_w"][0][None, :], (128, DM)), dtype=f),
        "n2bc": np.ascontiguousarray(np.broadcast_to(inputs["norm2_w"][0][None, :], (128, DM)), dtype=f),
        "nfbc": np.ascontiguousarray(np.broadcast_to(inputs["final_norm_w"][None, :], (128, DM)), dtype=f),
        "convw": np.ascontiguousarray(inputs["gdn_conv_w"][0].reshape(4, 12, 128).transpose(2, 1, 0).reshape(128, 48), dtype=f),
        "fbias_bc": np.ascontiguousarray(np.broadcast_to(inputs["fox_f_bias"][0][None, :], (128, 8)), dtype=f),
        "alog_bc": np.ascontiguousarray(np.broadcast_to(inputs["gdn_A_log"][0][None, :], (128, 8)), dtype=f),
        "dtb_bc": np.ascontiguousarray(np.broadcast_to(inputs["gdn_dt_bias"][0][None, :], (128, 8)), dtype=f),
        "gnorm_bc": np.ascontiguousarray(np.broadcast_to(inputs["gdn_out_norm_w"][0][None, :], (128, 64)), dtype=f),
        "fqw": np.ascontiguousarray(np.tile(inputs["fox_q_norm_w"][0], 2)[:, None], dtype=f),
        "fkw": np.ascontiguousarray(np.tile(inputs["fox_k_norm_w"][0], 2)[:, None], dtype=f),
    }
    shared.update(_consts())
    in_maps = []
    for b in range(8):
        m = dict(shared)
        m["x"] = x[b]
        in_maps.append(m)
    return in_maps


def kernel(**inputs):
    nc = build()
    in_maps = _host_inputs(inputs)
    res = run_bass_kernel_spmd(nc, in_maps, core_ids=list(range(8)))
    return np.stack([np.asarray(r["out"], dtype=np.float32) for r in res.results], axis=0)
```

```python
import contextlib
import numpy as np
import ml_dtypes
import concourse.bass as bass
import concourse.mybir as mybir
from concourse.bass_utils import run_bass_kernel_spmd

F32 = mybir.dt.float32
BF16 = mybir.dt.bfloat16
AF = mybir.ActivationFunctionType
ALU = mybir.AluOpType
AX = mybir.AxisListType

ENGS = ("pe", "act", "dve", "pool", "sp")
NDMA_SEMS = 32
NHW, NSW = 24, 8
_DTSZ = {F32: 4, BF16: 2}

T = 2048
DM = 1024
NT = 16
KC = 8
DFF = 2816
NFT = 22
EPS = 1e-6
NEG = -30000.0

C_GQ, C_GK, C_GV, C_GZ, C_GB, C_GA = 0, 512, 1024, 1536, 2048, 2056
C_FQ, C_FK, C_FV, C_FG, C_FF = 2064, 2576, 3088, 3600, 4112
DIN = 4120


def _region(ap):
    t = ap.tensor
    if "DRam" in type(t).__name__:
        return None
    sz = _DTSZ[ap.dtype]
    F = 1
    for s in t.shape[1:]:
        F *= int(s)
    off = int(ap.offset)
    p0 = off // F
    f0 = (off % F) * sz
    pn = 1
    ext = 0
    for (st, cnt) in ap.ap:
        st = int(st); cnt = int(cnt)
        if st != 0 and st % F == 0:
            pn = max(pn, (cnt - 1) * (st // F) + 1)
        else:
            ext += (cnt - 1) * abs(st)
    f1 = f0 + (ext + 1) * sz
    if "PSum" in type(t).__name__:
        return (t.name, 0, 128, (f0 // 2048) * 2048, ((f1 + 2047) // 2048) * 2048)
    return (t.name, p0, p0 + pn, f0, f1)


def _overlap(a, b):
    return a[0] == b[0] and a[1] < b[2] and b[1] < a[2] and a[3] < b[4] and b[3] < a[4]


def _contains(a, b):
    return a[0] == b[0] and a[1] <= b[1] and a[2] >= b[2] and a[3] <= b[3] and a[4] >= b[4]


class Op:
    __slots__ = ("eng", "meth", "args", "kw", "deps", "idx", "is_dma", "marked", "seq", "dsem", "dval", "reads", "writes")


class Prog:
    def __init__(self, nc):
        self.nc = nc
        self.ops = []
        self.hist = {}
        self.ndma = 0
        self.dma_ops = {"hw": [], "sw": []}
        self.capture = None

    def I(self, eng, meth, *args, **kw):
        if self.capture is not None:
            self.capture.append((eng, meth, args, kw))
            return None
        op = Op()
        op.eng = eng; op.meth = meth; op.args = args
        op.is_dma = meth == "dma_start"
        op.idx = len(self.ops)
        op.marked = False; op.seq = 0; op.dsem = None; op.dval = 0
        reads = []; writes = []
        extra_r = kw.pop("_reads", None)
        extra_w = kw.pop("_writes", None)
        extra_d = kw.pop("_deps", None)
        op.kw = kw
        for k, v in kw.items():
            if isinstance(v, bass.AP):
                r = _region(v)
                if r is None:
                    continue
                if k in ("out", "accum_out"):
                    writes.append(r)
                else:
                    reads.append(r)
        for v in (extra_r or []):
            r = _region(v)
            if r: reads.append(r)
        for v in (extra_w or []):
            r = _region(v)
            if r: writes.append(r)
        op.reads = reads; op.writes = writes
        deps = set()
        for d_ in (extra_d or []):
            deps.add((d_.idx, 3))
        for r in reads:
            psum_r = r[0] == "PS"
            for (rg, oi, isw) in self.hist.get(r[0], ()):
                if isw and _overlap(rg, r):
                    deps.add((oi, 0))
                elif psum_r and (not isw) and _overlap(rg, r) and self.ops[oi].eng != eng:
                    deps.add((oi, 4))
        for w in writes:
            h = self.hist.get(w[0], [])
            newh = []
            for rec in h:
                (rg, oi, isw) = rec
                if _overlap(rg, w):
                    deps.add((oi, 1 if isw else 2))
                    if _contains(w, rg):
                        continue
                newh.append(rec)
            newh.append((w, op.idx, True))
            self.hist[w[0]] = newh
        for r in reads:
            self.hist.setdefault(r[0], []).append((r, op.idx, False))
        if op.is_dma:
            self.ndma += 1
            kind_ = "sw" if eng == "pool" else "hw"
            lst = self.dma_ops[kind_]
            npool = NSW if kind_ == "sw" else NHW
            j = len(lst)
            if j >= npool:
                deps.add((lst[j - npool].idx, 3))
            lst.append(op)
            op.dsem = (j % npool) + (NHW if kind_ == "sw" else 0)
            op.dval = 16 * (j // npool + 1)
        fd = {}
        for (oi, kind) in deps:
            if oi == op.idx:
                continue
            p = self.ops[oi]
            same = (p.eng == op.eng) and (not p.is_dma) and (not op.is_dma)
            if same and op.eng == "pe":
                continue
            fd[oi] = True
        op.deps = sorted(fd.keys())
        for oi in op.deps:
            self.ops[oi].marked = True
        self.ops.append(op)
        return op

    def mm(self, out, lhsT, rhs, start=True, stop=True):
        return self.I("pe", "matmul", out=out, lhsT=lhsT, rhs=rhs, start=start, stop=stop)

    def tr(self, out, in_, ident):
        return self.I("pe", "transpose", out=out, in_=in_, identity=ident)

    def dma(self, q, out, in_, deps=None):
        return self.I(q, "dma_start", out=out, in_=in_, _deps=deps)

    def act(self, out, in_, func, **kw):
        return self.I("act", "activation", out=out, in_=in_, func=func, **kw)

    def tt(self, eng, out, in0, in1, op):
        return self.I(eng, "tensor_tensor", out=out, in0=in0, in1=in1, op=op)

    def ts(self, eng, out, in0, s1, op0, s2=None, op1=None):
        if op1 is None:
            return self.I(eng, "tensor_scalar", out=out, in0=in0, scalar1=s1, scalar2=None, op0=op0)
        return self.I(eng, "tensor_scalar", out=out, in0=in0, scalar1=s1, scalar2=s2, op0=op0, op1=op1)

    def stt(self, eng, out, in0, scalar, in1, op0, op1):
        return self.I(eng, "scalar_tensor_tensor", out=out, in0=in0, scalar=scalar, in1=in1, op0=op0, op1=op1)

    def cp(self, eng, out, in_):
        if eng == "act":
            return self.I("act", "copy", out=out, in_=in_)
        return self.I(eng, "tensor_copy", out=out, in_=in_)

    def memset(self, eng, ap, val):
        return self.I(eng, "memset", ap, val, _writes=[ap])

    def emit(self, final_wait_ops):
        nc = self.nc
        cnt = {e: 0 for e in ENGS}
        for op in self.ops:
            if op.is_dma:
                continue
            if op.marked:
                cnt[op.eng] += 1
                op.seq = cnt[op.eng]
        self.counts = dict(cnt); self.counts['dma'] = self.ndma; self.counts['ops'] = len(self.ops)
        with contextlib.ExitStack() as es:
            sems = {e: es.enter_context(nc.semaphore("s_" + e)) for e in ENGS}
            dsems = [es.enter_context(nc.semaphore("d%d" % i)) for i in range(NDMA_SEMS)]
            block = es.enter_context(nc.Block())
            ops = self.ops

            def run(engname, E):
                waited = {}
                for op in ops:
                    if op.eng != engname:
                        continue
                    for oi in op.deps:
                        p = ops[oi]
                        if p.is_dma:
                            key = ("d", p.dsem); val = p.dval; sem = dsems[p.dsem]
                        else:
                            key = ("e", p.eng); val = p.seq; sem = sems[p.eng]
                        if waited.get(key, 0) >= val:
                            continue
                        waited[key] = val
                        E.wait_ge(sem, val)
                    ins = getattr(E, op.meth)(*op.args, **op.kw)
                    if op.is_dma:
                        ins.then_inc(dsems[op.dsem], 16)
                    elif op.marked:
                        ins.then_inc(sems[op.eng], 1)
                if engname == "sp":
                    for op in final_wait_ops:
                        E.wait_ge(dsems[op.dsem], op.dval)

            @block.tensor
            def _(E): run("pe", E)

            @block.scalar
            def _(E): run("act", E)

            @block.vector
            def _(E): run("dve", E)

            @block.gpsimd
            def _(E): run("pool", E)

            @block.sync
            def _(E): run("sp", E)


def _consts():
    bf = ml_dtypes.bfloat16
    c = {}
    c["ident_bf"] = np.eye(128, dtype=np.float32).astype(bf)
    t = np.arange(128)
    c["tri_f"] = (t[:, None] <= t[None, :]).astype(np.float32)
    c["ones_f"] = np.ones((128, 128), np.float32)
    key = (np.arange(4)[None, :, None] * 128 + t[:, None, None])
    q = np.arange(512)[None, None, :]
    c["negmask4"] = np.where(key <= q, 0.0, NEG).astype(np.float32).astype(bf).reshape(128, 2048)
    incl = np.where(t[None, :] <= t[:, None], 0.0, NEG).astype(np.float32)
    c["negincl4"] = np.tile(incl, (1, 4)).astype(bf)
    c["strict01"] = (t[None, :] < t[:, None]).astype(np.float32).astype(bf)
    ep = np.zeros((40, 8, 128), np.float32)
    for r in list(range(8)) + list(range(32, 40)):
        h = r % 32
        ep[r, (h % 2) * 4 + h // 2, :] = 1.0
    c["epat"] = ep.reshape(40, 1024).astype(bf)
    c["nepat"] = (-ep).reshape(40, 1024).astype(bf)
    o40 = np.zeros((40, 128), np.float32)
    o40[0:8] = 1.0; o40[32:40] = 1.0
    c["ones40"] = o40.astype(bf)
    c["blk64"] = (t[:, None] // 64 == t[None, :] // 64).astype(np.float32).astype(bf)
    return c


_CONST_SPECS = [("ident_bf", [128, 128], BF16), ("tri_f", [128, 128], F32), ("ones_f", [128, 128], F32),
                ("negmask4", [128, 2048], BF16), ("negincl4", [128, 512], BF16), ("strict01", [128, 128], BF16),
                ("epat", [40, 1024], BF16), ("nepat", [40, 1024], BF16), ("ones40", [40, 128], BF16),
                ("blk64", [128, 128], BF16)]

_PARAM_SPECS = [("n1bc", [128, 1024]), ("n2bc", [128, 1024]), ("nfbc", [128, 1024]), ("convw", [128, 48]),
                ("fbias_bc", [128, 8]), ("alog_bc", [128, 8]), ("dtb_bc", [128, 8]), ("gnorm_bc", [128, 64]),
                ("fqw", [128, 1]), ("fkw", [128, 1])]


NDUMMY = 2
GDUMMY = 0


def build(debug=None, upto=None):
    nc = bass.Bass("TRN2", target_bir_lowering=False)
    dr = {}

    def din(name, shape, dt=F32):
        dr[name] = nc.dram_tensor(name, shape, dt, kind="ExternalInput").ap()
        return dr[name]

    x_d = din("x", [T, DM])
    w_in = din("w_in", [DM, DIN])
    w_out = din("w_out", [DM, DM])
    w_g = din("w_g", [DM, DFF])
    w_u = din("w_u", [DM, DFF])
    w_d = din("w_d", [DFF, DM])
    for (n, s) in _PARAM_SPECS:
        din(n, s)
    for (n, s, dt) in _CONST_SPECS:
        din(n, s, dt)
    out_d = nc.dram_tensor("out", [T, DM], F32, kind="ExternalOutput").ap()
    x1s = nc.dram_tensor("x1s", [T, DM], F32, kind="Internal").ap()
    dbg_d = {}
    for (n, s) in (debug or []):
        dbg_d[n] = nc.dram_tensor(n, s, F32, kind="ExternalOutput").ap()

    w_in_v = w_in.rearrange("(kc p) n -> p kc n", p=128)
    w_out_v = w_out.rearrange("(kc p) n -> p kc n", p=128)
    w_g_v = w_g.rearrange("(kc p) n -> p kc n", p=128)
    w_u_v = w_u.rearrange("(kc p) n -> p kc n", p=128)
    w_d_v = w_d.rearrange("(kc p) n -> p kc n", p=128)

    with contextlib.ExitStack() as es:
        def sb(name, shape, dt):
            return es.enter_context(nc.sbuf_tensor(name, shape, dt))

        P = Prog(nc)
        HT = sb("HT", [128, 16384], BF16)
        BIG1 = sb("BIG1", [128, 16384], BF16)
        BIG2 = sb("BIG2", [128, 16640], BF16)
        FMIX = sb("FMIX", [128, 8192], BF16)
        W16 = sb("W16", [128, 8192], BF16)
        A32 = sb("A32", [128, 8448], F32)
        PTB = sb("PTB", [128, 1536], BF16)
        JNK = sb("JNK", [128, 1024], BF16)
        SM = sb("SM", [128, 2048], F32)
        cst = {}
        for (n, s, dt) in _CONST_SPECS:
            cst[n] = sb("c_" + n, s, dt)
        prm = {}
        for (n, s) in _PARAM_SPECS:
            prm[n] = sb("p_" + n, s, F32)
        PS = es.enter_context(nc.psum_tensor("PS", [128, 4096], F32))

        def bank(b, n=1):
            return PS[:, b * 512:(b + n) * 512]

        def bank_bf(b, n=1):
            return PS[:, b * 512:(b + n) * 512].bitcast(BF16)

        ident = cst["ident_bf"]

        P.dma("sp", out=A32[:, 0:1024], in_=x_d[0:128, :])
        P.dma("act", out=prm["n1bc"][:], in_=dr["n1bc"][:, :])
        P.dma("act", out=cst["ident_bf"][:], in_=dr["ident_bf"][:, :])
        P.dma("sp", out=A32[:, 1024:2048], in_=x_d[128:256, :])
        qs = ["sp", "act"]
        k = 0
        for (n, s, dt) in _CONST_SPECS:
            if n != "ident_bf":
                P.dma(qs[k % 2], out=cst[n][:], in_=dr[n][:, :]); k += 1
        for (n, s) in _PARAM_SPECS:
            if n != "n1bc":
                P.dma(qs[k % 2], out=prm[n][:], in_=dr[n][:, :]); k += 1

        hT = HT[:, :].rearrange("p (kc t) -> p kc t", kc=KC)

        sm_off = [0]

        def smalloc(n):
            o = sm_off[0]
            sm_off[0] += n
            assert sm_off[0] <= 1536
            return SM[:, o:o + n]

        ss1 = smalloc(16); ln1 = smalloc(16); rstd1 = smalloc(16)
        logits = smalloc(NT * 24).rearrange("p (i c) -> p i c", i=NT)
        nlogf = smalloc(NT * 8).rearrange("p (i c) -> p i c", i=NT)
        beta = smalloc(NT * 8).rearrange("p (i c) -> p i c", i=NT)
        nbeta = smalloc(NT * 8).rearrange("p (i c) -> p i c", i=NT)
        nglog = smalloc(NT * 8).rearrange("p (i c) -> p i c", i=NT)
        tmp8a = smalloc(NT * 8).rearrange("p (i c) -> p i c", i=NT)
        tmp8b = smalloc(NT * 8).rearrange("p (i c) -> p i c", i=NT)
        aexp = smalloc(8)
        fqw_s = smalloc(1)

        xst = [A32[:, 0:1024], A32[:, 1024:2048]]
        hb = [JNK[:, :], PTB[:, 0:1024]]
        def a_s1(i):
            xs = xst[i % 2]
            if i >= 2:
                P.dma("sp", out=xs, in_=x_d[i * 128:(i + 1) * 128, :])
            P.act(out=A32[:, 2048:3072], in_=xs, func=AF.Square, accum_out=ss1[:, i:i + 1])
            P.act(out=ln1[:, i:i + 1], in_=ss1[:, i:i + 1], func=AF.Ln, scale=1.0 / DM, bias=EPS)
            P.act(out=rstd1[:, i:i + 1], in_=ln1[:, i:i + 1], func=AF.Exp, scale=-0.5)

        def a_s2(i):
            xs = xst[i % 2]
            h_b = hb[i % 2]
            P.stt("dve", out=h_b, in0=xs, scalar=rstd1[:, i:i + 1], in1=prm["n1bc"][:, :], op0=ALU.mult, op1=ALU.mult)
            pt = bank_bf(i % 2)
            for kc in range(KC):
                P.tr(pt[:, kc * 128:(kc + 1) * 128], h_b[:, kc * 128:(kc + 1) * 128], ident[:, :])
            P.cp("act" if i % 2 == 0 else "dve", out=hT[:, :, i * 128:(i + 1) * 128],
                 in_=pt.rearrange("p (kc t) -> p kc t", kc=KC))

        a_s1(0)
        for i in range(NT):
            if i + 1 < NT:
                a_s1(i + 1)
            a_s2(i)

        wbuf = [W16[:, 0:4096].rearrange("p (kc n) -> p kc n", kc=KC),
                W16[:, 4096:8192].rearrange("p (kc n) -> p kc n", kc=KC)]
        wsm = PTB[:, 1024:1024 + 192].rearrange("p (kc n) -> p kc n", kc=KC)
        P.dma("pool", out=wsm[:, :, 0:16], in_=w_in_v[:, :, C_GB:C_GB + 16])
        P.dma("pool", out=wsm[:, :, 16:24], in_=w_in_v[:, :, C_FF:C_FF + 8])
        P.dma("pool", out=wbuf[0], in_=w_in_v[:, :, C_FV:C_FV + 512])
        P.dma("pool", out=wbuf[1], in_=w_in_v[:, :, C_FG:C_FG + 512])
        for i in range(NT):
            pl = bank(2 + i % 2)[:, 0:24]
            for kc in range(KC):
                P.mm(pl, hT[:, kc, i * 128:(i + 1) * 128], wsm[:, kc, :], start=(kc == 0), stop=(kc == KC - 1))
            P.cp("dve", out=logits[:, i, :], in_=pl)
        fb_bc = prm["fbias_bc"][:, :].unsqueeze(1).broadcast_to([128, NT, 8])
        dtb_bc = prm["dtb_bc"][:, :].unsqueeze(1).broadcast_to([128, NT, 8])
        P.tt("dve", out=tmp8a, in0=logits[:, :, 16:24], in1=fb_bc, op=ALU.add)
        P.act(out=tmp8a, in_=tmp8a, func=AF.Exp, scale=-1.0)
        P.act(out=nlogf, in_=tmp8a, func=AF.Ln, bias=1.0)
        P.act(out=tmp8b, in_=logits[:, :, 0:8], func=AF.Exp, scale=-1.0)
        P.act(out=tmp8b, in_=tmp8b, func=AF.Ln, bias=1.0)
        P.act(out=beta, in_=tmp8b, func=AF.Exp, scale=-1.0)
        P.ts("dve", out=nbeta, in0=beta, s1=-1.0, op0=ALU.mult)
        P.tt("dve", out=tmp8a, in0=logits[:, :, 8:16], in1=dtb_bc, op=ALU.add)
        P.act(out=tmp8a, in_=tmp8a, func=AF.Exp)
        P.act(out=tmp8a, in_=tmp8a, func=AF.Ln, bias=1.0)
        P.act(out=aexp, in_=prm["alog_bc"][:, :], func=AF.Exp)
        P.tt("dve", out=nglog, in0=tmp8a, in1=aexp.unsqueeze(1).broadcast_to([128, NT, 8]), op=ALU.mult)
        P.ts("dve", out=fqw_s, in0=prm["fqw"][:, :], s1=0.125, op0=ALU.mult)

        nF = A32[0:8, 0:2048]
        frow = A32[0:8, 2048:6144].bitcast(BF16)
        KH = frow[:, 0:2048]; KL = frow[:, 2048:4096]; QH = frow[:, 4096:6144]; QL = frow[:, 6144:8192]
        for i in range(NT):
            pf = bank(4 + i % 2)[0:8, 0:128]
            P.mm(pf, nlogf[:, i, :], cst["tri_f"][:, :])
            if i == 0:
                P.cp("dve", out=nF[:, 0:128], in_=pf)
            else:
                P.ts("dve", out=nF[:, i * 128:(i + 1) * 128], in0=pf, s1=nF[:, i * 128 - 1:i * 128], op0=ALU.add)
        P.cp("dve", out=KH, in_=nF)
        P.tt("dve", out=KL, in0=nF, in1=KH, op=ALU.subtract)
        P.ts("dve", out=QH, in0=KH, s1=-1.0, op0=ALU.mult)
        P.ts("dve", out=QL, in0=KL, s1=-1.0, op0=ALU.mult)

        fv_aug = BIG2[:, 0:NT * 8 * 65].rearrange("p (i h e) -> p i h e", i=NT, h=8)
        gate_sig = BIG2[:, 8448:8448 + 8192].rearrange("p (i c) -> p i c", i=NT)
        P.memset("pool", fv_aug[:, :, :, 64:65], 1.0)
        for i in range(NT):
            pv = bank(2 + i % 2)
            for kc in range(KC):
                P.mm(pv, hT[:, kc, i * 128:(i + 1) * 128], wbuf[0][:, kc, :], start=(kc == 0), stop=(kc == KC - 1))
            P.cp("dve", out=fv_aug[:, i, :, 0:64], in_=pv.rearrange("p (h e) -> p h e", h=8))
        for i in range(NT):
            pg = bank(2 + i % 2)
            for kc in range(KC):
                P.mm(pg, hT[:, kc, i * 128:(i + 1) * 128], wbuf[1][:, kc, :], start=(kc == 0), stop=(kc == KC - 1))
            P.act(out=gate_sig[:, i, :], in_=pg, func=AF.Sigmoid)

        P.dma("pool", out=wbuf[0], in_=w_in_v[:, :, C_FQ:C_FQ + 512])
        P.dma("pool", out=wbuf[1], in_=w_in_v[:, :, C_FK:C_FK + 512])
        aug = BIG1[:, :].rearrange("p (s qk a t) -> p s qk a t", s=2, qk=2, a=2)
        P.memset("pool", BIG1[64:68, :], 1.0)
        fmix = FMIX[:, :].rearrange("p (i c) -> p i c", i=NT)
        sqb = PTB[:, 1024:1536]
        lnv = A32[:, 4096:4608] if False else SM[:, 1536:2048]
        rsv = A32[:, 0:512]
        PT = [PTB[:, 0:512], PTB[:, 512:1024]]
        rec = smalloc(8)
        nrec = [0]
        STB = [bank(0), bank(1), bank(2)]
        PT3 = [PTB[:, 0:512], PTB[:, 512:1024], PTB[:, 1024:1536]]
        sqb2 = [JNK[:, 0:512], JNK[:, 512:1024]]
        for hp in range(4):
            slot = hp % 2
            groups = [(qk, tb) for qk in range(2) for tb in range(4)]

            def b3_mm(g):
                qk, tb = groups[g]
                pq = bank(3 + g % 2)
                for kc in range(KC):
                    P.mm(pq, wbuf[qk][:, kc, hp * 128:(hp + 1) * 128], hT[:, kc, tb * 512:(tb + 1) * 512],
                         start=(kc == 0), stop=(kc == KC - 1))

            def b3_sq(g):
                P.act(out=sqb2[g % 2], in_=bank(3 + g % 2), func=AF.Square)

            def b3_ss(g):
                P.mm(bank(5 + g % 2), cst["blk64"][:, :], sqb2[g % 2])

            def b3_fin(g):
                qk, tb = groups[g]
                pq = bank(3 + g % 2); pss = bank(5 + g % 2)
                nw = fqw_s if qk == 0 else prm["fkw"][:, :]
                P.act(out=pss, in_=pss, func=AF.Ln, scale=1.0 / 64, bias=EPS)
                P.act(out=lnv, in_=pss, func=AF.Exp, scale=-0.5)
                for a in range(2):
                    P.stt("dve", out=aug[0:64, slot, qk, a, tb * 512:(tb + 1) * 512],
                          in0=pq[a * 64:(a + 1) * 64, :], scalar=nw[a * 64:(a + 1) * 64, 0:1],
                          in1=lnv[a * 64:(a + 1) * 64, :], op0=ALU.mult, op1=ALU.mult)

            ng_ = len(groups)
            b3_mm(0); b3_sq(0)
            for g in range(ng_):
                if g + 1 < ng_:
                    b3_mm(g + 1)
                b3_ss(g)
                if g + 1 < ng_:
                    b3_sq(g + 1)
                b3_fin(g)
            for a in range(2):
                h = hp * 2 + a
                P.dma("sp", out=aug[64:65, slot, 1, a, :], in_=KH[h:h + 1, :])
                P.dma("sp", out=aug[65:66, slot, 1, a, :], in_=KL[h:h + 1, :])
                P.dma("sp", out=aug[66:67, slot, 0, a, :], in_=QH[h:h + 1, :])
                P.dma("sp", out=aug[67:68, slot, 0, a, :], in_=QL[h:h + 1, :])
            items = [(a, I, j) for a in range(2) for I in range(4) for j in range(4 * I + 4)]

            def stS(idx):
                a, I, j = items[idx]
                qa = aug[0:68, slot, 0, a, :]; ka = aug[0:68, slot, 1, a, :]
                st = STB[idx % 3]
                r = max(0, j - 4 * I)
                diag = j >= 4 * I
                P.mm(st[:, r * 128:512], ka[:, j * 128:(j + 1) * 128], qa[:, I * 512 + r * 128:(I + 1) * 512], start=True, stop=not diag)
                if diag:
                    P.mm(st[:, r * 128:(r + 1) * 128], ident[:, :], cst["negmask4"][:, 0:128], start=False, stop=True)

            def stE(idx):
                a, I, j = items[idx]
                r = max(0, j - 4 * I)
                P.act(out=PT3[idx % 3][:, r * 128:512], in_=STB[idx % 3][:, r * 128:512], func=AF.Exp)

            def stV(idx):
                a, I, j = items[idx]
                h = hp * 2 + a
                pt_ = PT3[idx % 3]
                for ip in range(max(j, 4 * I), 4 * I + 4):
                    r = ip - 4 * I
                    po = bank(3 + r)[:, 0:65]
                    P.mm(po, pt_[:, r * 128:(r + 1) * 128], fv_aug[:, j, h, :], start=(j == 0), stop=(j == ip))
                    if j == ip:
                        rc = rec[:, nrec[0] % 8:nrec[0] % 8 + 1]
                        nrec[0] += 1
                        P.I("dve", "reciprocal", out=rc, in_=po[:, 64:65])
                        P.stt("dve", out=fmix[:, ip, h * 64:(h + 1) * 64], in0=po[:, 0:64], scalar=rc,
                              in1=gate_sig[:, ip, h * 64:(h + 1) * 64], op0=ALU.mult, op1=ALU.mult)

            n_ = len(items)
            for idx in range(n_ + 2):
                if idx < n_:
                    stS(idx)
                if 0 <= idx - 1 < n_:
                    stE(idx - 1)
                if 0 <= idx - 2 < n_:
                    stV(idx - 2)
                for _ in range(NDUMMY):
                    P.mm(bank(7)[:, 0:256], ident[:, :], cst["negmask4"][:, 0:256])

        qnT = BIG1[:, 0:8192].rearrange("p (c t) -> p c t", c=4)
        knT = BIG1[:, 8192:16384].rearrange("p (c t) -> p c t", c=4)
        vT = BIG2[:, 0:8192].rearrange("p (c t) -> p c t", c=4)
        zs_tok = BIG2[:, 8192:16384].rearrange("p (i c) -> p i c", i=NT)
        P.dma("pool", out=wbuf[0], in_=w_in_v[:, :, C_GZ:C_GZ + 512])
        P.dma("pool", out=wbuf[1], in_=w_in_v[:, :, C_GQ:C_GQ + 512])
        for i in range(NT if upto != 'B' else 0):
            pz = bank(i % 2)
            for kc in range(KC):
                P.mm(pz, hT[:, kc, i * 128:(i + 1) * 128], wbuf[0][:, kc, :], start=(kc == 0), stop=(kc == KC - 1))
            P.act(out=zs_tok[:, i, :], in_=pz, func=AF.Silu)
            P.tt("pool", out=zs_tok[:, i, :].rearrange("p (h e) -> p h e", h=8), in0=zs_tok[:, i, :].rearrange("p (h e) -> p h e", h=8),
                 in1=prm["gnorm_bc"][:, :].unsqueeze(1).broadcast_to([128, 8, 64]), op=ALU.mult)
        raws = [A32[:, 0:2051], A32[:, 2051:4102]]
        accs = [A32[:, 4102:6150], A32[:, 6150:8198]]
        P.memset("pool", A32[:, 0:3], 0.0)
        P.memset("pool", A32[:, 2051:2054], 0.0)
        cw = prm["convw"]
        NCT = 12 if upto != 'B' else 0

        def c2_s1c(ct, tb):
            grp = ct // 4
            wb = wbuf[(grp + 1) % 2]
            if tb == 0 and ct == 0:
                P.dma("pool", out=wbuf[0], in_=w_in_v[:, :, C_GK:C_GK + 512])
            if tb == 0 and ct == 4:
                P.dma("pool", out=wbuf[1], in_=w_in_v[:, :, C_GV:C_GV + 512])
            c0 = (ct % 4) * 128
            raw = raws[ct % 2]
            pp = bank(2 + tb % 2)
            for kc in range(KC):
                P.mm(pp, wb[:, kc, c0:c0 + 128], hT[:, kc, tb * 512:(tb + 1) * 512], start=(kc == 0), stop=(kc == KC - 1))
            P.cp("act", out=raw[:, 3 + tb * 512:3 + (tb + 1) * 512], in_=pp)

        def c2_tap(ct, n):
            raw = raws[ct % 2]; acc = accs[ct % 2]
            if n == 0:
                P.ts("dve", out=acc, in0=raw[:, 3:2051], s1=cw[:, ct * 4 + 3:ct * 4 + 4], op0=ALU.mult)
            else:
                kk = 3 - n
                P.stt("dve", out=acc, in0=raw[:, kk:kk + 2048], scalar=cw[:, ct * 4 + kk:ct * 4 + kk + 1],
                      in1=acc, op0=ALU.mult, op1=ALU.add)

        def c2_silu(ct):
            acc = accs[ct % 2]
            if ct >= 8:
                P.act(out=vT[:, ct - 8, :], in_=acc, func=AF.Silu)
            else:
                P.act(out=acc, in_=acc, func=AF.Silu)

        def n_sq(ct, tb):
            a_ = accs[ct % 2][:, tb * 512:(tb + 1) * 512]
            P.tt("pool", out=sqb2[tb % 2], in0=a_, in1=a_, op=ALU.mult)

        def n_ss(ct, tb):
            P.mm(bank(4 + tb % 2), cst["blk64"][:, :], sqb2[tb % 2])

        def n_fin(ct, tb):
            acc = accs[ct % 2]
            sl = slice(tb * 512, (tb + 1) * 512)
            pss = bank(4 + tb % 2)
            P.act(out=pss, in_=pss, func=AF.Ln, bias=EPS)
            P.act(out=pss, in_=pss, func=AF.Exp, scale=-0.5)
            if ct < 4:
                P.stt("dve", out=qnT[:, ct, sl], in0=acc[:, sl], scalar=0.125, in1=pss, op0=ALU.mult, op1=ALU.mult)
            else:
                P.tt("dve", out=knT[:, ct - 4, sl], in0=acc[:, sl], in1=pss, op=ALU.mult)

        for it in range(NCT + 2 if NCT else 0):
            cn = it - 1; nn = it - 2
            nv = 0 <= nn < 8
            if nv:
                n_sq(nn, 0)
            for tb in range(4):
                if it < NCT:
                    c2_s1c(it, tb)
                if nv:
                    n_ss(nn, tb)
                    if tb < 3:
                        n_sq(nn, tb + 1)
                    n_fin(nn, tb)
                if 0 <= cn < NCT:
                    c2_tap(cn, tb)
            if 0 <= cn < NCT:
                c2_silu(cn)

        NTI = NT if upto not in ('C2', 'B') else 0
        if upto and upto.startswith('C3:'):
            NTI = int(upto.split(':')[1])
        wout = W16[:, :].rearrange("p (kc n) -> p kc n", kc=KC)
        P.dma("pool", out=wout, in_=w_out_v[:, :, :])

        def htv(o, n):
            return HT[:, o:o + n]
        KBG = htv(0, 512); KD = htv(512, 512); VB = htv(1024, 512)
        GS = HT[0:40, 1536:1664]; GEXP = HT[0:40, 1664:2688]
        DINC = htv(2688, 1024); DS = htv(3712, 1024); Am = htv(4736, 1024); AT = htv(5760, 1024)
        Pbuf = [htv(6784, 1024), htv(7808, 1024)]
        Qbuf = [htv(8832, 1024), htv(9856, 1024)]
        Ybuf = [htv(10880, 1024), htv(11904, 1024)]
        WT = htv(12928, 512); VNEW = htv(13440, 512); SBF = htv(13952, 256)
        GM = htv(14208, 512); MIXT = htv(14720, 1024)
        xs_t = A32[:, 0:1024]; x1o = A32[:, 1024:2048]
        u32 = A32[:, 2048:2560]; o32 = A32[:, 2560:3072]; otmp = A32[:, 3072:3584]
        S32 = A32[:, 3584:3840]
        ngpad = A32[:, 3840:4480].rearrange("p (i c) -> p i c", i=NT)
        ng = smalloc(8); dlt = smalloc(8); eg = smalloc(8); edl = smalloc(8); egl = smalloc(8); bg = smalloc(8)
        ssq = smalloc(8); lno = smalloc(8)
        P.memset("pool", A32[:, 3840:4480], 0.0)
        P.cp("pool", out=ngpad[:, :, 0:8], in_=nglog)
        P.cp("pool", out=ngpad[:, :, 32:40], in_=nglog)
        P.memset("pool", S32, 0.0)
        P.memset("pool", SBF, 0.0)

        def HX(h):
            return (h % 2) * 4 + h // 2

        def v8(ap, e):
            return ap.rearrange("p (h e) -> p h e", h=8)

        def bc_last(ap8, e):
            return ap8.unsqueeze(2).broadcast_to([128, 8, e])

        x1w = {}
        SUB = int(upto.split(':')[2]) if (upto and upto.count(':') == 2) else 99

        def gdn_tile(i):
            tsl = slice(i * 128, (i + 1) * 128)
            kv = bank_bf(0)
            for hp in range(4):
                P.tr(kv[:, hp * 128:(hp + 1) * 128], knT[:, hp, tsl], ident[:, :])
            for hp in range(4):
                P.tr(kv[:, 512 + hp * 128:512 + (hp + 1) * 128], vT[:, hp, tsl], ident[:, :])
            kps = kv[:, 0:512]; vps = kv[:, 512:1024]
            if SUB < 2:
                return
            b1 = bank(1)
            pg = b1[:, 0:8]; pgl = b1[:, 8:16]; pgt = b1[0:40, 128:256]
            P.mm(pg, cst["tri_f"][:, :], nglog[:, i, :])
            P.mm(pgl, cst["ones_f"][:, :], nglog[:, i, :])
            P.mm(pgt, ngpad[:, i, :], cst["tri_f"][:, :])
            P.cp("dve", out=ng, in_=pg)
            P.act(out=eg, in_=pg, func=AF.Exp, scale=-1.0)
            P.tt("dve", out=dlt, in0=pgl, in1=ng, op=ALU.subtract)
            P.act(out=edl, in_=dlt, func=AF.Exp, scale=-1.0)
            P.act(out=egl, in_=pgl, func=AF.Exp, scale=-1.0)
            P.tt("dve", out=bg, in0=beta[:, i, :], in1=eg, op=ALU.mult)
            if SUB < 3:
                return
            P.tt("dve", out=v8(KBG, 64), in0=v8(kps, 64), in1=bc_last(bg, 64), op=ALU.mult)
            P.tt("dve", out=v8(KD, 64), in0=v8(kps, 64), in1=bc_last(edl, 64), op=ALU.mult)
            P.tt("dve", out=v8(VB, 64), in0=v8(vps, 64), in1=bc_last(beta[:, i, :], 64), op=ALU.mult)
            if SUB < 4:
                return
            P.cp("dve", out=GS, in_=pgt)
            P.tt("dve", out=GS[32:40, :], in0=pgt[32:40, :], in1=GS[32:40, :], op=ALU.subtract)
            P.tt("dve", out=GEXP.rearrange("p (h m) -> p h m", h=8), in0=GS.unsqueeze(1).broadcast_to([40, 8, 128]),
                 in1=cst["epat"][:, :].rearrange("p (h m) -> p h m", h=8), op=ALU.mult)
            ds4 = DS.rearrange("p (a hp m) -> p a hp m", a=2, hp=4)
            nb4 = nbeta[:, i, :].rearrange("p (hp a) -> p a hp", a=2).unsqueeze(3).broadcast_to([128, 2, 4, 128])
            P.tt("pool", out=ds4, in0=cst["strict01"][:, :].unsqueeze(1).unsqueeze(1).broadcast_to([128, 2, 4, 128]), in1=nb4, op=ALU.mult)
            if SUB < 5:
                return
            yield
            for b in range(2):
                pd = bank(2 + b)
                P.mm(pd, GS, cst["nepat"][:, b * 512:(b + 1) * 512], start=True, stop=False)
                P.mm(pd, cst["ones40"][:, :], GEXP[:, b * 512:(b + 1) * 512], start=False, stop=False)
                P.mm(pd, ident[:, :], cst["negincl4"][:, :], start=False, stop=True)
            P.act(out=DINC, in_=bank(2, 2), func=AF.Exp)
            if SUB < 6:
                return
            pG = bank(4, 2); pQK = bank(6, 2)
            for h in range(8):
                hr = slice((h % 2) * 64, (h % 2) * 64 + 64)
                P.mm(pG[:, HX(h) * 128:(HX(h) + 1) * 128], knT[hr, h // 2, tsl], knT[hr, h // 2, tsl])
            for h in range(8):
                hr = slice((h % 2) * 64, (h % 2) * 64 + 64)
                P.mm(pQK[:, HX(h) * 128:(HX(h) + 1) * 128], qnT[hr, h // 2, tsl], knT[hr, h // 2, tsl])
            if SUB < 7:
                return
            P.tt("dve", out=Am, in0=pQK, in1=DINC, op=ALU.mult)
            P.tt("dve", out=DS, in0=DS, in1=DINC, op=ALU.mult)
            Pc, Pn = Pbuf[0], Pbuf[1]
            Qc, Qn = Qbuf[0], Qbuf[1]
            Yc, Yn = Ybuf[0], Ybuf[1]
            P.tt("dve", out=Pc, in0=pG, in1=DS, op=ALU.mult)
            blk8 = cst["blk64"][:, :].unsqueeze(1).broadcast_to([128, 8, 128])
            P.tt("dve", out=v8(Pn, 128), in0=v8(Pc, 128), in1=blk8, op=ALU.mult)
            POFF = DINC
            P.tt("dve", out=POFF, in0=Pc, in1=Pn, op=ALU.subtract)
            Pc, Pn = Pn, Pc
            pt0 = bank_bf(0); pt1 = bank_bf(1)
            for h in range(8):
                P.tr(pt0[:, h * 128:(h + 1) * 128], Am[:, h * 128:(h + 1) * 128], ident[:, :])
            P.cp("act", out=AT, in_=pt0)
            for h in range(8):
                P.tr(pt1[:, h * 128:(h + 1) * 128], Pc[:, h * 128:(h + 1) * 128], ident[:, :])
            P.cp("act", out=Qc, in_=pt1)
            P.tt("dve", out=v8(Yc, 128), in0=v8(pt1, 128), in1=ident[:, :].unsqueeze(1).broadcast_to([128, 8, 128]), op=ALU.add)
            if SUB < 8:
                return
            for k in range(6):
                needQ = k <= 3
                needP = k <= 4
                needY = k >= 1
                for b in range(2):
                    bq = bank(2 + 3 * b); bp = bank(3 + 3 * b); by = bank(4 + 3 * b)
                    hs = slice(b * 512, (b + 1) * 512)
                    for hh in range(4):
                        h = 4 * b + hh
                        s_ = slice(h * 128, (h + 1) * 128)
                        o_ = slice(hh * 128, (hh + 1) * 128)
                        if needQ:
                            P.mm(bq[:, o_], Pc[:, s_], Qc[:, s_])
                        if needY:
                            P.mm(by[:, o_], Pc[:, s_], Yc[:, s_])
                        if needP:
                            P.mm(bp[:, o_], Qc[:, s_], Pc[:, s_])
                    for _ in range(GDUMMY):
                        P.mm(bank(0)[:, 0:128], ident[:, :], cst["negmask4"][:, 0:128])
                    if needQ:
                        P.cp("act", out=Qn[:, hs], in_=bq)
                    if needP:
                        P.cp("act", out=Pn[:, hs], in_=bp)
                    if needY:
                        P.tt("dve", out=Yn[:, hs], in0=by, in1=Yc[:, hs], op=ALU.add)
                if needQ:
                    Qc, Qn = Qn, Qc
                if needP:
                    Pc, Pn = Pn, Pc
                if needY:
                    Yc, Yn = Yn, Yc
            Ybd = Yc
            ptb = bank_bf(2)
            for h in range(8):
                P.tr(ptb[:, h * 128:(h + 1) * 128], Ybd[:, h * 128:(h + 1) * 128], ident[:, :])
            TBD = Qbuf[0]; M1 = Qbuf[1]
            pm1 = bank(4, 2); pm2 = bank(6, 2)
            for b in range(2):
                hs = slice(b * 512, (b + 1) * 512)
                P.cp("dve", out=TBD[:, hs], in_=ptb[:, hs])
                for hh in range(4):
                    s_ = slice((4 * b + hh) * 128, (4 * b + hh + 1) * 128)
                    P.mm(pm1[:, s_], POFF[:, s_], Ybd[:, s_])
                P.cp("act", out=M1[:, hs], in_=pm1[:, hs])
            for b in range(2):
                hs = slice(b * 512, (b + 1) * 512)
                for hh in range(4):
                    s_ = slice((4 * b + hh) * 128, (4 * b + hh + 1) * 128)
                    P.mm(pm2[:, s_], TBD[:, s_], M1[:, s_])
                P.tt("dve", out=Ybd[:, hs], in0=pm2[:, hs], in1=Ybd[:, hs], op=ALU.add)
            Y = Ybd
            if SUB < 9:
                return
            pu = bank(0); pw = bank(1)
            for h in range(8):
                P.mm(pu[:, h * 64:(h + 1) * 64], Y[:, HX(h) * 128:(HX(h) + 1) * 128], VB[:, h * 64:(h + 1) * 64])
            P.cp("act", out=u32, in_=pu)
            for h in (0, 2, 4, 6, 1, 3, 5, 7):
                hr = slice((h % 2) * 64, (h % 2) * 64 + 64)
                P.mm(pw[hr, (h // 2) * 128:(h // 2 + 1) * 128], KBG[:, h * 64:(h + 1) * 64], Y[:, HX(h) * 128:(HX(h) + 1) * 128])
            P.cp("act", out=WT, in_=pw)
            if SUB < 10:
                return
            pws = [bank(2)[:, 0:256], bank(3)[:, 0:256]]; po1 = [bank(6)[:, 0:256], bank(7)[:, 0:256]]
            po2 = bank(4); pS = bank(5)[:, 0:256]
            for h in (0, 2, 4, 6, 1, 3, 5, 7):
                hr = slice((h % 2) * 64, (h % 2) * 64 + 64)
                P.mm(pws[h % 2][:, (h // 2) * 64:(h // 2 + 1) * 64], WT[hr, (h // 2) * 128:(h // 2 + 1) * 128], SBF[hr, (h // 2) * 64:(h // 2 + 1) * 64])
            u4 = u32.rearrange("p (hp a e) -> p hp a e", hp=4, a=2)
            vn4 = VNEW.rearrange("p (hp a e) -> p hp a e", hp=4, a=2)
            for a in range(2):
                P.tt("dve", out=vn4[:, :, a, :], in0=u4[:, :, a, :], in1=pws[a].rearrange("p (hp e) -> p hp e", hp=4), op=ALU.subtract)
            for h in (0, 2, 4, 6, 1, 3, 5, 7):
                hr = slice((h % 2) * 64, (h % 2) * 64 + 64)
                P.mm(po1[h % 2][:, (h // 2) * 64:(h // 2 + 1) * 64], qnT[hr, h // 2, tsl], SBF[hr, (h // 2) * 64:(h // 2 + 1) * 64])
            for h in range(8):
                P.mm(po2[:, h * 64:(h + 1) * 64], AT[:, HX(h) * 128:(HX(h) + 1) * 128], VNEW[:, h * 64:(h + 1) * 64])
            for h in (0, 2, 4, 6, 1, 3, 5, 7):
                hr = slice((h % 2) * 64, (h % 2) * 64 + 64)
                P.mm(pS[hr, (h // 2) * 64:(h // 2 + 1) * 64], KD[:, h * 64:(h + 1) * 64], VNEW[:, h * 64:(h + 1) * 64])
            ot4 = otmp.rearrange("p (hp a e) -> p hp a e", hp=4, a=2)
            eg4 = eg.rearrange("p (hp a) -> p hp a", a=2)
            for a in range(2):
                P.tt("dve", out=ot4[:, :, a, :], in0=po1[a].rearrange("p (hp e) -> p hp e", hp=4),
                     in1=eg4[:, :, a].unsqueeze(2).broadcast_to([128, 4, 64]), op=ALU.mult)
            P.tt("dve", out=o32, in0=otmp, in1=po2, op=ALU.add)
            eglv = egl.rearrange("p (hp a) -> p hp a", a=2)
            for a in range(2):
                pr = slice(a * 64, (a + 1) * 64)
                P.tt("pool", out=S32[pr, :].rearrange("p (hp e) -> p hp e", hp=4), in0=S32[pr, :].rearrange("p (hp e) -> p hp e", hp=4),
                     in1=eglv[pr, :, a].unsqueeze(2).broadcast_to([64, 4, 64]), op=ALU.mult)
            P.tt("dve", out=S32, in0=S32, in1=pS, op=ALU.add)
            P.cp("act", out=SBF, in_=S32)
            if SUB < 11:
                return
            yield
            P.tt("dve", out=otmp, in0=o32, in1=o32, op=ALU.mult)
            P.I("dve", "reduce_sum", out=ssq, in_=v8(otmp, 64), axis=AX.X)
            P.act(out=lno, in_=ssq, func=AF.Ln, scale=1.0 / 64, bias=EPS)
            P.act(out=lno, in_=lno, func=AF.Exp, scale=-0.5)
            P.tt("dve", out=v8(otmp, 64), in0=v8(o32, 64), in1=bc_last(lno, 64), op=ALU.mult)
            P.tt("dve", out=GM, in0=otmp, in1=zs_tok[:, i, :], op=ALU.mult)
            if SUB < 12:
                return
            yield
            pm = bank_bf(2)
            for kc in range(4):
                P.tr(pm[:, kc * 128:(kc + 1) * 128], GM[:, kc * 128:(kc + 1) * 128], ident[:, :])
            for kc in range(4):
                P.tr(pm[:, (4 + kc) * 128:(5 + kc) * 128], fmix[:, i, kc * 128:(kc + 1) * 128], ident[:, :])
            P.cp("act", out=MIXT, in_=pm)
            P.dma("sp", out=xs_t, in_=x_d[i * 128:(i + 1) * 128, :])
            py = bank(6, 2)
            for nb in range(2):
                for kc in range(KC):
                    P.mm(py[:, nb * 512:(nb + 1) * 512], MIXT[:, kc * 128:(kc + 1) * 128], wout[:, kc, nb * 512:(nb + 1) * 512],
                         start=(kc == 0), stop=(kc == KC - 1))
            P.tt("dve", out=x1o, in0=py, in1=xs_t, op=ALU.add)
            x1w[i] = P.dma("sp", out=x1s[i * 128:(i + 1) * 128, :], in_=x1o)
            if debug and any(n == "d_gm" for (n, s) in debug):
                stg = A32[:, 4480:4992]
                P.cp("dve", out=stg, in_=GM)
                P.dma("sp", out=dbg_d["d_gm"][i * 128:(i + 1) * 128, :], in_=stg)
                stg2 = A32[:, 4992:5504]
                P.cp("dve", out=stg2, in_=o32)
                P.dma("sp", out=dbg_d["d_o32"][i * 128:(i + 1) * 128, :], in_=stg2)

        gens = [gdn_tile(i) for i in range(NTI)]

        def adv(g):
            try:
                next(g)
            except StopIteration:
                pass

        if NTI:
            adv(gens[0])
        for i in range(NTI):
            adv(gens[i])
            P.capture = []; adv(gens[i]); la = P.capture
            lb = []
            if i + 1 < NTI:
                P.capture = []; adv(gens[i + 1]); lb = P.capture
            P.capture = None
            for k_ in range(max(len(la), len(lb))):
                if k_ < len(lb):
                    e_, m_, a_, kw_ = lb[k_]; P.I(e_, m_, *a_, **kw_)
                if k_ < len(la):
                    e_, m_, a_, kw_ = la[k_]; P.I(e_, m_, *a_, **kw_)
            adv(gens[i])

        X1B = A32[:, 0:8192].rearrange("p (t c) -> p t c", t=8)
        OUTB = FMIX[:, :].bitcast(F32)[:, 0:1024]
        H2T = W16[:, :].rearrange("p (kc t) -> p kc t", kc=KC)

        def ACTT(f):
            if f < 16:
                return HT[:, f * 1024:(f + 1) * 1024]
            return BIG2[:, (f - 16) * 1024:(f - 15) * 1024]
        wgb = [BIG1[:, 0:4096].rearrange("p (kc n) -> p kc n", kc=KC), BIG1[:, 8192:12288].rearrange("p (kc n) -> p kc n", kc=KC)]
        wub = [BIG1[:, 4096:8192].rearrange("p (kc n) -> p kc n", kc=KC), BIG1[:, 12288:16384].rearrange("p (kc n) -> p kc n", kc=KC)]
        wdk = [BIG2[:, 6144:10240].rearrange("p (kc n) -> p kc n", kc=8), BIG2[:, 10240:14336].rearrange("p (kc n) -> p kc n", kc=8),
               FMIX[:, 2048:5120].rearrange("p (kc n) -> p kc n", kc=6)]
        sgs2 = [SM[:, 1536:2048], FMIX[:, :].bitcast(F32)[:, 2560:3072]]
        ss2 = smalloc(16); ln2 = smalloc(16); ss3 = smalloc(16); ln3 = smalloc(16)
        final_ops = []
        cntg = 0; cntp = 0
        XSTG = FMIX[:, :].bitcast(F32)[:, 3072:4096]

        def f_s1(tb_, t8, src):
            i = tb_ * 8 + t8
            P.act(out=hb[i % 2], in_=src, func=AF.Square, accum_out=ss2[:, i:i + 1])
            P.act(out=ln2[:, i:i + 1], in_=ss2[:, i:i + 1], func=AF.Ln, scale=1.0 / DM, bias=EPS)
            P.act(out=ln2[:, i:i + 1], in_=ln2[:, i:i + 1], func=AF.Exp, scale=-0.5)

        def f_s2(tb_, t8, src):
            i = tb_ * 8 + t8
            h_b = hb[i % 2]
            P.stt("dve", out=h_b, in0=src, scalar=ln2[:, i:i + 1], in1=prm["n2bc"][:, :], op0=ALU.mult, op1=ALU.mult)
            pt = bank_bf(i % 2)
            for kc in range(KC):
                P.tr(pt[:, kc * 128:(kc + 1) * 128], h_b[:, kc * 128:(kc + 1) * 128], ident[:, :])
            P.cp("act" if i % 2 == 0 else "dve", out=H2T[:, :, t8 * 128:(t8 + 1) * 128],
                 in_=pt.rearrange("p (kc t) -> p kc t", kc=KC))

        def f_next_tile(t8):
            i = 8 + t8
            P.dma("sp", out=XSTG, in_=x1s[i * 128:(i + 1) * 128, :], deps=[x1w[i]])
            f_s1(1, t8, XSTG)
            f_s2(1, t8, XSTG)

        for tb in range(2 if upto is None else 0):
            for t8 in range(8):
                i = tb * 8 + t8
                P.dma("sp" if t8 % 2 == 0 else "act", out=X1B[:, t8, :], in_=x1s[i * 128:(i + 1) * 128, :], deps=[x1w[i]])
            if tb == 0:
                f_s1(0, 0, X1B[:, 0, :])
                for t8 in range(8):
                    if t8 + 1 < 8:
                        f_s1(0, t8 + 1, X1B[:, t8 + 1, :])
                    f_s2(0, t8, X1B[:, t8, :])
            nxt_tiles = list(range(8)) if tb == 0 else []
            for fg in range(6):
                ncol = 512 if fg < 5 else 256
                wg_ = wgb[cntg % 2]; wu_ = wub[cntg % 2]; cntg += 1
                P.dma("pool", out=wg_[:, :, 0:ncol], in_=w_g_v[:, :, fg * 512:fg * 512 + ncol])
                P.dma("pool", out=wu_[:, :, 0:ncol], in_=w_u_v[:, :, fg * 512:fg * 512 + ncol])
                for ft in range(ncol // 128):
                    f = fg * 4 + ft
                    for th in range(2):
                        tsl_ = slice(th * 512, (th + 1) * 512)
                        psg = bank(0 + 2 * (cntp % 2)); psu = bank(1 + 2 * (cntp % 2))
                        sg_ = sgs2[cntp % 2]
                        cntp += 1
                        for kc in range(KC):
                            P.mm(psg, wg_[:, kc, ft * 128:(ft + 1) * 128], H2T[:, kc, tsl_], start=(kc == 0), stop=(kc == KC - 1))
                        for kc in range(KC):
                            P.mm(psu, wu_[:, kc, ft * 128:(ft + 1) * 128], H2T[:, kc, tsl_], start=(kc == 0), stop=(kc == KC - 1))
                        P.act(out=sg_, in_=psg, func=AF.Silu)
                        P.tt("dve", out=ACTT(f)[:, tsl_], in0=sg_, in1=psu, op=ALU.mult)
            def ffn_final(t8):
                i = tb * 8 + t8
                P.act(out=OUTB, in_=X1B[:, t8, :], func=AF.Square, accum_out=ss3[:, i:i + 1])
                P.act(out=ln3[:, i:i + 1], in_=ss3[:, i:i + 1], func=AF.Ln, scale=1.0 / DM, bias=EPS)
                P.act(out=ln3[:, i:i + 1], in_=ln3[:, i:i + 1], func=AF.Exp, scale=-0.5)
                P.stt("dve", out=OUTB, in0=X1B[:, t8, :], scalar=ln3[:, i:i + 1], in1=prm["nfbc"][:, :], op0=ALU.mult, op1=ALU.mult)
                o = P.dma("sp", out=out_d[i * 128:(i + 1) * 128, :], in_=OUTB)
                final_ops.append(o)

            for nb in range(2):
                for kg in range(3):
                    nk = 8 if kg < 2 else 6
                    P.dma("pool", out=wdk[kg], in_=w_d_v[:, kg * 8:kg * 8 + nk, nb * 512:(nb + 1) * 512])
                for half in range(2):
                    for kg in range(3):
                        nk = 8 if kg < 2 else 6
                        for t4 in range(4):
                            tt_ = half * 4 + t4
                            for j in range(nk):
                                f = kg * 8 + j
                                P.mm(bank(4 + t4), ACTT(f)[:, tt_ * 128:(tt_ + 1) * 128], wdk[kg][:, j, :], start=(f == 0), stop=(f == NFT - 1))
                        if nxt_tiles:
                            f_next_tile(nxt_tiles.pop(0))
                    for t4 in range(4):
                        tt_ = half * 4 + t4
                        P.tt("dve", out=X1B[:, tt_, nb * 512:(nb + 1) * 512], in0=bank(4 + t4), in1=X1B[:, tt_, nb * 512:(nb + 1) * 512], op=ALU.add)
                    if nb == 1:
                        for t4 in range(4):
                            ffn_final(half * 4 + t4)
        if debug:
            for (n, s) in debug:
                if n == "d_fmix":
                    for i in range(NT):
                        stg = A32[:, 5120:5632]
                        P.cp("dve", out=stg, in_=fmix[:, i, :])
                        final_ops.append(P.dma("sp", out=dbg_d[n][i * 128:(i + 1) * 128, :], in_=stg))
                if n == "d_x1":
                    for i in sorted(x1w.keys()):
                        stg = A32[:, 5120:6144]
                        P.dma("sp", out=stg, in_=x1s[i * 128:(i + 1) * 128, :], deps=[x1w[i]])
                        o = P.dma("sp", out=dbg_d[n][i * 128:(i + 1) * 128, :], in_=stg)
                        final_ops.append(o)
        if upto is not None:
            final_ops.append(P.dma('sp', out=out_d[0:128, :], in_=A32[:, 0:1024]))
        P.emit(final_ops)
        nc._prog_counts = P.counts
        nc._prog = P
    return nc


def _host_inputs(inputs):
    f = np.float32
    x = np.ascontiguousarray(inputs["x"], dtype=f)
    shared = {
        "w_in": np.ascontiguousarray(inputs["w_in"][0], dtype=f),
        "w_out": np.ascontiguousarray(inputs["w_out"][0], dtype=f),
        "w_g": np.ascontiguousarray(inputs["w_ffn_gate"][0], dtype=f),
        "w_u": np.ascontiguousarray(inputs["w_ffn_up"][0], dtype=f),
        "w_d": np.ascontiguousarray(inputs["w_ffn_down"][0], dtype=f),
        "n1bc": np.ascontiguousarray(np.broadcast_to(inputs["norm1_w"][0][None, :], (128, DM)), dtype=f),
        "n2bc": np.ascontiguousarray(np.broadcast_to(inputs["norm2_w"][0][None, :], (128, DM)), dtype=f),
        "nfbc": np.ascontiguousarray(np.broadcast_to(inputs["final_norm_w"][None, :], (128, DM)), dtype=f),
        "convw": np.ascontiguousarray(inputs["gdn_conv_w"][0].reshape(4, 12, 128).transpose(2, 1, 0).reshape(128, 48), dtype=f),
        "fbias_bc": np.ascontiguousarray(np.broadcast_to(inputs["fox_f_bias"][0][None, :], (128, 8)), dtype=f),
        "alog_bc": np.ascontiguousarray(np.broadcast_to(inputs["gdn_A_log"][0][None, :], (128, 8)), dtype=f),
        "dtb_bc": np.ascontiguousarray(np.broadcast_to(inputs["gdn_dt_bias"][0][None, :], (128, 8)), dtype=f),
        "gnorm_bc": np.ascontiguousarray(np.broadcast_to(inputs["gdn_out_norm_w"][0][None, :], (128, 64)), dtype=f),
        "fqw": np.ascontiguousarray(np.tile(inputs["fox_q_norm_w"][0], 2)[:, None], dtype=f),
        "fkw": np.ascontiguousarray(np.tile(inputs["fox_k_norm_w"][0], 2)[:, None], dtype=f),
    }
    shared.update(_consts())
    in_maps = []
    for b in range(8):
        m = dict(shared)
        m["x"] = x[b]
        in_maps.append(m)
    return in_maps


def kernel(**inputs):
    nc = build()
    in_maps = _host_inputs(inputs)
    res = run_bass_kernel_spmd(nc, in_maps, core_ids=list(range(8)))
    return np.stack([np.asarray(r["out"], dtype=np.float32) for r in res.results], axis=0)
```
